# Optimizing a Trainium2 kernel written in Bass

```python
import math
import jax, jax.numpy as jnp
from jax import lax
import numpy as np

D_MODEL = 2048
BATCH = 8
SEQ = 2048
DEPTH = 1

N_ATT_HEADS = 8
ATT_HEAD_DIM = D_MODEL // (4 * N_ATT_HEADS)
ATT_V_DIM = 2 * ATT_HEAD_DIM
ATT_WIDTH = N_ATT_HEADS * ATT_V_DIM
QK_COLS = N_ATT_HEADS * 2 * ATT_HEAD_DIM
ROPE_THETA = 10000.0
Q_BLOCK = 128
SGU_WIDTH = D_MODEL // 2
N_SGU_GROUPS = 8
SGU_GROUP_DIM = SGU_WIDTH // N_SGU_GROUPS
CHUNK = 128
N_BRANCHES = 2
GATE_COLS = N_BRANCHES * D_MODEL
IN_COLS = QK_COLS + QK_COLS + ATT_WIDTH + SGU_WIDTH + SGU_WIDTH + GATE_COLS
D_FF = 256 * ((8 * D_MODEL // 3 + 255) // 256)
CONV_WIDTH = 3
EPS = 1e-6

kernel_name = "hybrid_diffattn_sgu_convffn"


def rms_norm(x, g):
    xf = x.astype(jnp.float32)
    y = xf * lax.rsqrt(jnp.mean(xf * xf, axis=-1, keepdims=True) + EPS)
    return (y * g.astype(jnp.float32)).astype(x.dtype)


def layer_norm(x, g, b):
    xf = x.astype(jnp.float32)
    mu = jnp.mean(xf, axis=-1, keepdims=True)
    xc = xf - mu
    y = xc * lax.rsqrt(jnp.mean(xc * xc, axis=-1, keepdims=True) + EPS)
    return (y * g.astype(jnp.float32) + b.astype(jnp.float32)).astype(x.dtype)


def rope_tables(seq, dim):
    inv = jnp.exp(-math.log(ROPE_THETA) * jnp.arange(0, dim, 2, dtype=jnp.float32) / dim)
    ang = jnp.arange(seq, dtype=jnp.float32)[:, None] * inv[None, :]
    return jnp.cos(ang), jnp.sin(ang)


def apply_rope(x, cos, sin):
    x1, x2 = jnp.split(x.astype(jnp.float32), 2, axis=-1)
    c = cos[None, :, None, None, :]
    s = sin[None, :, None, None, :]
    return jnp.concatenate([x1 * c - x2 * s, x1 * s + x2 * c], axis=-1).astype(x.dtype)


def diff_attention(q, k, v, lam):
    seq = q.shape[1]
    scale = ATT_HEAD_DIM ** -0.5
    outs = []
    for blk in range(seq // Q_BLOCK):
        q0 = blk * Q_BLOCK
        kv_len = q0 + Q_BLOCK
        qb = q[:, q0:kv_len]
        kb = k[:, :kv_len]
        vb = v[:, :kv_len]
        s = jnp.einsum('bqhcd,bkhcd->bhcqk', qb, kb,
                       preferred_element_type=jnp.float32) * scale
        mask = (q0 + jnp.arange(Q_BLOCK))[:, None] >= jnp.arange(kv_len)[None, :]
        p = jax.nn.softmax(jnp.where(mask, s, -jnp.inf), axis=-1)
        a = p[:, :, 0] - lam * p[:, :, 1]
        outs.append(jnp.einsum('bhqk,bkhd->bqhd', a.astype(v.dtype), vb))
    return jnp.concatenate(outs, axis=1)


def spatial_gating(u, v, ln_g, ln_b, w_s, b_s):
    bsz, seq, _ = v.shape
    v = layer_norm(v, ln_g, ln_b)
    vc = v.reshape(bsz, seq // CHUNK, CHUNK, N_SGU_GROUPS, SGU_GROUP_DIM)
    causal = jnp.tril(jnp.ones((CHUNK, CHUNK), dtype=bool))
    w = jnp.where(causal[None], w_s, jnp.zeros_like(w_s)).astype(v.dtype)
    s = jnp.einsum('gtp,bnpgc->bntgc', w, vc) + b_s.T.astype(v.dtype)[None, None, :, :, None]
    return u * s.reshape(bsz, seq, SGU_WIDTH)


def causal_dwconv(x, w, b):
    ch = x.shape[-1]
    y = lax.conv_general_dilated(x, w[:, None, :].astype(x.dtype), window_strides=(1,),
                                 padding=[(CONV_WIDTH - 1, 0)],
                                 dimension_numbers=('NWC', 'WIO', 'NWC'),
                                 feature_group_count=ch)
    return y + b.astype(x.dtype)


def setup_inputs(seed: int = 0) -> dict:
    key = jax.random.key(seed)
    ks = jax.random.split(key, 24)
    f32 = jnp.float32
    nrm = lambda k, shape, scale: jax.random.normal(k, shape, f32) * scale
    L = DEPTH
    return {
        "x": nrm(ks[0], (BATCH, SEQ, D_MODEL), 1.0),
        "norm1_g": 1.0 + nrm(ks[1], (L, D_MODEL), 0.02),
        "w_in": nrm(ks[2], (L, D_MODEL, IN_COLS), D_MODEL ** -0.5),
        "b_gate": nrm(ks[3], (L, GATE_COLS), 0.02),
        "q_norm_g": 1.0 + nrm(ks[4], (L, ATT_HEAD_DIM), 0.02),
        "k_norm_g": 1.0 + nrm(ks[5], (L, ATT_HEAD_DIM), 0.02),
        "lambda_q1": nrm(ks[6], (L, ATT_HEAD_DIM), 0.1),
        "lambda_k1": nrm(ks[7], (L, ATT_HEAD_DIM), 0.1),
        "lambda_q2": nrm(ks[8], (L, ATT_HEAD_DIM), 0.1),
        "lambda_k2": nrm(ks[9], (L, ATT_HEAD_DIM), 0.1),
        "subln_g": 1.0 + nrm(ks[10], (L, ATT_V_DIM), 0.02),
        "sgu_norm_g": 1.0 + nrm(ks[11], (L, SGU_WIDTH), 0.02),
        "sgu_norm_b": nrm(ks[12], (L, SGU_WIDTH), 0.02),
        "sgu_w": nrm(ks[13], (L, N_SGU_GROUPS, CHUNK, CHUNK), 0.5 * CHUNK ** -0.5),
        "sgu_b": 1.0 + nrm(ks[14], (L, N_SGU_GROUPS, CHUNK), 0.02),
        "w_att_out": nrm(ks[15], (L, ATT_WIDTH, D_MODEL), ATT_WIDTH ** -0.5),
        "w_sgu_out": nrm(ks[16], (L, SGU_WIDTH, D_MODEL), SGU_WIDTH ** -0.5),
        "w_out": nrm(ks[17], (L, D_MODEL, D_MODEL), D_MODEL ** -0.5),
        "norm2_g": 1.0 + nrm(ks[18], (L, D_MODEL), 0.02),
        "w_up": nrm(ks[19], (L, D_MODEL, 2 * D_FF), D_MODEL ** -0.5),
        "conv_w": nrm(ks[20], (L, CONV_WIDTH, 2 * D_FF), CONV_WIDTH ** -0.5),
        "conv_b": nrm(ks[21], (L, 2 * D_FF), 0.02),
        "w_down": nrm(ks[22], (L, D_FF, D_MODEL), D_FF ** -0.5),
    }


def reference(x, norm1_g, w_in, b_gate, q_norm_g, k_norm_g, lambda_q1, lambda_k1,
              lambda_q2, lambda_k2, subln_g, sgu_norm_g, sgu_norm_b, sgu_w, sgu_b,
              w_att_out, w_sgu_out, w_out, norm2_g, w_up, conv_w, conv_b, w_down):
    bsz, seq, _ = x.shape
    cos, sin = rope_tables(seq, ATT_HEAD_DIM)
    split_points = [QK_COLS, 2 * QK_COLS, 2 * QK_COLS + ATT_WIDTH,
                    2 * QK_COLS + ATT_WIDTH + SGU_WIDTH,
                    2 * QK_COLS + ATT_WIDTH + 2 * SGU_WIDTH]
    h = x
    for l in range(DEPTH):
        a = rms_norm(h, norm1_g[l])
        z = a @ w_in[l]
        q, k, v, u, sv, gates = jnp.split(z, split_points, axis=-1)

        q = apply_rope(rms_norm(q.reshape(bsz, seq, N_ATT_HEADS, 2, ATT_HEAD_DIM), q_norm_g[l]), cos, sin)
        k = apply_rope(rms_norm(k.reshape(bsz, seq, N_ATT_HEADS, 2, ATT_HEAD_DIM), k_norm_g[l]), cos, sin)
        v = v.reshape(bsz, seq, N_ATT_HEADS, ATT_V_DIM)
        lam_init = 0.8 - 0.6 * math.exp(-0.3 * l)
        lam = (jnp.exp(jnp.sum(lambda_q1[l].astype(jnp.float32) * lambda_k1[l].astype(jnp.float32)))
               - jnp.exp(jnp.sum(lambda_q2[l].astype(jnp.float32) * lambda_k2[l].astype(jnp.float32)))
               + lam_init)
        o = diff_attention(q, k, v, lam)
        o = rms_norm(o, subln_g[l]) * (1.0 - lam_init)
        att_branch = o.reshape(bsz, seq, ATT_WIDTH) @ w_att_out[l]

        sgu = spatial_gating(jax.nn.gelu(u, approximate=False), jax.nn.gelu(sv, approximate=False),
                             sgu_norm_g[l], sgu_norm_b[l], sgu_w[l], sgu_b[l])
        sgu_branch = sgu @ w_sgu_out[l]

        g_att, g_sgu = jnp.split(jax.nn.sigmoid(gates + b_gate[l]), N_BRANCHES, axis=-1)
        h = h + (g_att * att_branch + g_sgu * sgu_branch) @ w_out[l]

        c = rms_norm(h, norm2_g[l])
        up = causal_dwconv(c @ w_up[l], conv_w[l], conv_b[l])
        gate_ff, val_ff = jnp.split(up, 2, axis=-1)
        h = h + (jax.nn.gelu(gate_ff, approximate=False) * val_ff) @ w_down[l]
    return h
```

```python
import contextlib
import math
import numpy as np
import concourse.bass as bass
import concourse.mybir as mybir
from concourse.bass_utils import run_bass_kernel_spmd

F32 = mybir.dt.float32
BF16 = mybir.dt.bfloat16
AF = mybir.ActivationFunctionType
ALU = mybir.AluOpType

D = 2048
SEQ = 2048
T = 512
NTILE = 4
INC = 9216
DFF = 5632
NHC = 44
EPS = 1e-6
NS = 4
NTF = 5
NTB = 6
COMPUTE = ("pe", "act", "dve", "pool")

PP_N1G, PP_N2G, PP_BG, PP_GQA, PP_GQB, PP_GKA, PP_GKB, PP_SUBG, PP_LNG, PP_LNB, PP_CW, PP_CB, PP_END = (
    0, 16, 32, 64, 65, 66, 67, 68, 69, 77, 85, 349, 437)


class Sched:
    def __init__(self):
        self.ops = []
        self.last_writer = {}
        self.readers = {}

    def add(self, eng, fn, reads=(), writes=(), dma_key=None):
        idx = len(self.ops)
        deps = set()
        psr = [b for b in reads if isinstance(b, tuple) and b[0] == "ps"]
        if psr:
            reads = [b for b in reads if b not in psr]
            writes = list(writes) + [b for b in psr if b not in writes]
        for b in reads:
            w = self.last_writer.get(b)
            if w is not None:
                deps.add(w)
        for b in writes:
            w = self.last_writer.get(b)
            if w is not None:
                deps.add(w)
            for r in self.readers.get(b, ()):
                deps.add(r)
        for b in reads:
            self.readers.setdefault(b, []).append(idx)
        for b in writes:
            self.last_writer[b] = idx
            self.readers[b] = []
        self.ops.append(dict(eng=eng, fn=fn, deps=deps, dma_key=dma_key, idx=idx))
        return idx

    def emit(self, nc):
        ops = self.ops
        needed = set()
        for o in ops:
            nd = set()
            best = {}
            for d in o["deps"]:
                p = ops[d]
                if p["dma_key"] is not None:
                    nd.add(d)
                    continue
                if p["eng"] == "pe" and o["eng"] == "pe" and o["dma_key"] is None:
                    continue
                if best.get(p["eng"], -1) < d:
                    best[p["eng"]] = d
            nd |= set(best.values())
            o["deps"] = nd
            needed |= nd
        eng_count = {e: 0 for e in COMPUTE}
        key_count = {}
        for o in ops:
            if o["dma_key"] is not None:
                k = o["dma_key"]
                key_count[k] = key_count.get(k, 0) + 16
                o["sem"] = ("dma", k)
                o["cnt"] = key_count[k]
            elif o["idx"] in needed:
                e = o["eng"]
                eng_count[e] += 1
                o["sem"] = ("eng", e)
                o["cnt"] = eng_count[e]
            else:
                o["sem"] = None
        sem_names = [("eng", e) for e in COMPUTE] + [("dma", k) for k in key_count]
        with contextlib.ExitStack() as st:
            sems = {}
            for n_, sn in enumerate(sem_names):
                sems[sn] = st.enter_context(nc.semaphore(f"sem{n_}"))
            block = st.enter_context(nc.Block())
            per_eng = {e: [o for o in ops if o["eng"] == e] for e in COMPUTE + ("sp",)}

            def run(engobj, ename):
                waited = {}
                for o in per_eng[ename]:
                    need = {}
                    for d in o["deps"]:
                        p = ops[d]
                        s = p["sem"]
                        if need.get(s, 0) < p["cnt"]:
                            need[s] = p["cnt"]
                    for s, c in need.items():
                        if waited.get(s, 0) >= c:
                            continue
                        engobj.wait_ge(sems[s], c)
                        waited[s] = c
                    ins = o["fn"](engobj)
                    if o["sem"] is not None:
                        ins.then_inc(sems[o["sem"]], 16 if o["sem"][0] == "dma" else 1)
                if ename == "sp":
                    for k, c in key_count.items():
                        engobj.wait_ge(sems[("dma", k)], c)

            @block.tensor
            def _(e):
                run(e, "pe")

            @block.scalar
            def _(e):
                run(e, "act")

            @block.vector
            def _(e):
                run(e, "dve")

            @block.gpsimd
            def _(e):
                run(e, "pool")

            @block.sync
            def _(e):
                run(e, "sp")


def build(ntiles=NTILE, stop_after=None, debug=False):
    nc = bass.Bass("TRN2", target_bir_lowering=False)

    def din(name, shape):
        return nc.dram_tensor(name, list(shape), F32, kind="ExternalInput").ap()

    x_d = din("x", [SEQ, D])
    w_in_d = din("w_in", [D, INC])
    w_att_d = din("w_att_out", [1024, D])
    w_sgu_d = din("w_sgu_out", [1024, D])
    w_out_d = din("w_out", [D, D])
    w_up_d = din("w_up", [D, 2 * DFF])
    w_down_d = din("w_down", [DFF, D])
    pp_d = din("pp", [128, PP_END])
    lamb_d = din("lamb", [128, 256])
    cmf_d = din("cmf", [128, 5 * 128])
    wT_d = din("sgu_wT", [128, 8 * 128])
    bs_d = din("sgu_bs", [128, 8 * 128])
    cos_d = din("cosT", [128, SEQ])
    sin_d = din("sinT", [128, SEQ])
    out_d = nc.dram_tensor("out", [SEQ, D], F32, kind="ExternalOutput").ap()
    dbg_d = {}

    S = Sched()
    st = contextlib.ExitStack()
    with st:
        def sb(name, shape, dt):
            return st.enter_context(nc.sbuf_tensor("sb_" + name, list(shape), dt))

        kT = sb("kT", [128, 8, SEQ], BF16)
        vv = sb("vv", [128, 16, 1024], BF16)
        ar = sb("ar", [128, 48, 512], BF16)
        hh = sb("hh", [128, 4, D], F32)
        wr = sb("wr", [128, NS, 4096], BF16)
        cm = sb("cm", [128, 5, 128], BF16)
        pp = sb("pp", [128, PP_END], F32)
        cs = sb("cs", [128, 2, 512], F32)
        wTm = sb("wTm", [128, 8, 128], BF16)
        b2 = sb("b2", [128, 8, 128], F32)
        hb = sb("hb", [128, 32], F32)
        sc = sb("sc", [128, 16], F32)
        stt = sb("stt", [128, 64], F32)
        carry = sb("carry", [128, 88, 2], F32)
        tf = sb("tf", [128, NTF, 512], F32)
        tb = sb("tb", [128, NTB, 512], BF16)
        osqb = sb("osqb", [128, 512], BF16)
        ps = st.enter_context(nc.psum_tensor("ps", [128, 8, 512], F32))

        IDENT, ROT, BONES, TRI, ONES = range(5)
        SC_GQA, SC_GQB, SC_GKA, SC_GKB, SC_SUBG, SC_NLAM, SC_EPS, SC_T0, SC_T1, SC_T2, SC_T3, SC_M0, SC_M1 = range(13)

        def col(t, k):
            return t[:, k:k + 1]

        cnt = dict(bank=0, tf=0, tb=0, pair=0)

        def nb():
            b = cnt["bank"] % 8
            cnt["bank"] += 1
            return b

        def ntf():
            i = cnt["tf"] % NTF
            cnt["tf"] += 1
            return i

        def ntb():
            i = cnt["tb"] % NTB
            cnt["tb"] += 1
            return i

        def PS(b):
            return ("ps", b)

        def SL(i):
            return ("sl", i)

        def HK(c):
            return [("h", c, q) for q in range(4)]

        plan = []

        def plan_tile():
            for nm, base in (("q", 0), ("k", 1024), ("v", 2048), ("u", 3072), ("sv", 4096)):
                for b in range(4):
                    c0 = base + 256 * b
                    plan.append((f"{nm}{b}", w_in_d[:, c0:c0 + 256].rearrange("(k p) n -> p k n", p=128), 16, 256))
            for jp in range(8):
                c0 = 5120 + 256 * jp
                plan.append((f"ga{jp}", w_in_d[:, c0:c0 + 256].rearrange("(k p) n -> p k n", p=128), 16, 256))
                c0 = 7168 + 256 * jp
                plan.append((f"gs{jp}", w_in_d[:, c0:c0 + 256].rearrange("(k p) n -> p k n", p=128), 16, 256))
                plan.append((f"att{jp}", w_att_d[:, 256 * jp:256 * jp + 256].rearrange("(k p) n -> p k n", p=128), 8, 256))
                plan.append((f"sgu{jp}", w_sgu_d[:, 256 * jp:256 * jp + 256].rearrange("(k p) n -> p k n", p=128), 8, 256))
            for b in range(8):
                plan.append((f"wo{b}", w_out_d[:, 256 * b:256 * b + 256].rearrange("(k p) n -> p k n", p=128), 16, 256))
            for g in range(3):
                nch = 16 if g < 2 else 12
                for pr in range(nch // 2):
                    j0 = 16 * g + 2 * pr
                    plan.append((f"upg{j0}", w_up_d[:, 128 * j0:128 * j0 + 256].rearrange("(k p) n -> p k n", p=128), 16, 256))
                    plan.append((f"upv{j0}", w_up_d[:, DFF + 128 * j0:DFF + 128 * j0 + 256].rearrange("(k p) n -> p k n", p=128), 16, 256))
                for cb in range(4):
                    for hf in range(nch // 8 + (1 if nch % 8 else 0)):
                        k0 = 16 * g + 8 * hf
                        nk = min(8, 16 * g + nch - k0)
                        plan.append((f"dn{g}_{cb}_{hf}",
                                     w_down_d[128 * k0:128 * (k0 + nk), 512 * cb:512 * cb + 512].rearrange("(k p) n -> p k n", p=128),
                                     nk, 512))

        for _ in range(ntiles):
            plan_tile()
        wst = dict(issued=0, next=0, released=[False] * len(plan))

        def wview(slot, kd, ncol):
            return wr[:, slot, 0:kd * ncol].rearrange("p (k n) -> p k n", k=kd)

        def wpump():
            while wst["issued"] < len(plan):
                n = wst["issued"]
                if n >= NS and not wst["released"][n - NS]:
                    break
                tag, src, kd, ncol = plan[n]
                slot = n % NS
                dst = wview(slot, kd, ncol)
                S.add("pool", lambda e, dst=dst, src=src: e.dma_start(out=dst, in_=src),
                      writes=[("w", slot)], dma_key=("w", slot))
                wst["issued"] += 1

        def wacq(tag):
            n = wst["next"]
            assert plan[n][0] == tag, (plan[n][0], tag)
            wpump()
            assert wst["issued"] > n, "weight ring deadlock"
            wst["next"] += 1
            _, _, kd, ncol = plan[n]
            return n, n % NS, wview(n % NS, kd, ncol)

        def wrel(n):
            wst["released"][n] = True
            wpump()

        tff = tf[:].rearrange("p a b -> p (a b)")
        S.add("sp", lambda e: e.dma_start(out=pp[:], in_=pp_d), writes=["pp"], dma_key="c0")
        S.add("sp", lambda e: e.dma_start(out=tff[:, 0:640], in_=cmf_d), writes=[("tf", 0), ("tf", 1)], dma_key="c1")
        S.add("sp", lambda e: e.dma_start(out=b2[:].rearrange("p a b -> p (a b)"), in_=bs_d), writes=["b2"], dma_key="c2")
        S.add("sp", lambda e: e.dma_start(out=tff[:, 1024:2048], in_=wT_d), writes=[("tf", 2), ("tf", 3)], dma_key="c3")
        S.add("sp", lambda e: e.dma_start(out=tff[:, 2048:2304], in_=lamb_d), writes=[("tf", 4)], dma_key="c4")
        wpump()
        S.add("dve", lambda e: e.tensor_copy(out=cm[:].rearrange("p a b -> p (a b)"), in_=tff[:, 0:640]),
              reads=[("tf", 0), ("tf", 1)], writes=["cm"])
        S.add("dve", lambda e: e.memset(carry[:], 0.0), writes=["carry"])
        S.add("dve", lambda e: e.memset(col(sc, SC_EPS), EPS), writes=["sc_eps"])
        S.add("dve", lambda e: e.memset(sc[:, SC_M0:SC_M1 + 1], 0.0), writes=["sc_m"])
        S.add("dve", lambda e: e.memset(sc[0:64, SC_M0:SC_M0 + 1], 1.0), reads=["sc_m"], writes=["sc_m"])
        S.add("dve", lambda e: e.memset(sc[64:128, SC_M1:SC_M1 + 1], 1.0), reads=["sc_m"], writes=["sc_m"])
        for g in range(8):
            S.add("dve", lambda e, g=g: e.tensor_tensor(out=wTm[:, g, :], in0=tff[:, 1024 + 128 * g:1024 + 128 * g + 128],
                                                        in1=tff[:, TRI * 128:TRI * 128 + 128], op=ALU.mult),
                  reads=[("tf", 0), ("tf", 1), ("tf", 2), ("tf", 3)], writes=["wTm"])
        for g in range(8):
            bk = nb()
            S.add("pe", lambda e, g=g, bk=bk: e.matmul(ps[:, bk, 0:128], lhsT=cm[:, ONES, :], rhs=wTm[:, g, :], start=True, stop=True),
                  reads=["cm", "wTm"], writes=[PS(bk)])
            S.add("dve", lambda e, g=g, bk=bk: e.scalar_tensor_tensor(out=b2[:, g, :], in0=ps[:, bk, 0:128], scalar=col(pp, PP_LNB + g),
                                                                      in1=b2[:, g, :], op0=ALU.mult, op1=ALU.add),
                  reads=[PS(bk), "pp", "b2"], writes=["b2"])
        S.add("dve", lambda e: e.tensor_scalar(out=hb[:], in0=pp[:, PP_BG:PP_BG + 32], scalar1=0.5, scalar2=None, op0=ALU.mult),
              reads=["pp"], writes=["hb"])
        S.add("dve", lambda e: e.tensor_scalar(out=sc[:, SC_GQA:SC_GQB + 1], in0=pp[:, PP_GQA:PP_GQB + 1], scalar1=0.125, scalar2=None, op0=ALU.mult),
              reads=["pp"], writes=["sc_g"])
        S.add("dve", lambda e: e.tensor_copy(out=sc[:, SC_GKA:SC_GKB + 1], in_=pp[:, PP_GKA:PP_GKB + 1]),
              reads=["pp"], writes=["sc_g"])
        S.add("dve", lambda e: e.tensor_scalar(out=col(sc, SC_SUBG), in0=col(pp, PP_SUBG), scalar1=0.8, scalar2=None, op0=ALU.mult),
              reads=["pp"], writes=["sc_g"])
        S.add("dve", lambda e: e.tensor_tensor(out=tff[:, 2048:2112], in0=tff[:, 2048:2112], in1=tff[:, 2112:2176], op=ALU.mult),
              reads=[("tf", 4)], writes=[("tf", 4)])
        S.add("dve", lambda e: e.tensor_tensor(out=tff[:, 2176:2240], in0=tff[:, 2176:2240], in1=tff[:, 2240:2304], op=ALU.mult),
              reads=[("tf", 4)], writes=[("tf", 4)])
        S.add("dve", lambda e: e.reduce_sum(out=col(sc, SC_T0), in_=tff[:, 2048:2112], axis=mybir.AxisListType.X),
              reads=[("tf", 4)], writes=["sc_t"])
        S.add("dve", lambda e: e.reduce_sum(out=col(sc, SC_T1), in_=tff[:, 2176:2240], axis=mybir.AxisListType.X),
              reads=[("tf", 4)], writes=["sc_t"])
        S.add("act", lambda e: e.activation(out=sc[:, SC_T0:SC_T1 + 1], in_=sc[:, SC_T0:SC_T1 + 1], func=AF.Exp),
              reads=["sc_t"], writes=["sc_t"])
        S.add("dve", lambda e: e.tensor_tensor(out=col(sc, SC_T2), in0=col(sc, SC_T1), in1=col(sc, SC_T0), op=ALU.subtract),
              reads=["sc_t"], writes=["sc_t2"])
        S.add("dve", lambda e: e.tensor_scalar(out=col(sc, SC_NLAM), in0=col(sc, SC_T2), scalar1=-0.2, scalar2=None, op0=ALU.add),
              reads=["sc_t2"], writes=["sc_nlam"])
        SCALL = ["sc_g", "sc_nlam", "sc_eps"]

        def norm_to_T(gcol0, tagrd):
            for c in range(4):
                junk = ar[:, 40:44, :].rearrange("p a b -> p (a b)")
                S.add("act", lambda e, c=c, junk=junk: e.activation(out=junk, in_=hh[:, c, :], func=AF.Square, accum_out=col(stt, c)),
                      reads=HK(c), writes=[SL(40), SL(41), SL(42), SL(43), ("stt", c)])
                S.add("act", lambda e, c=c: e.activation(out=col(stt, 4 + c), in_=col(stt, c), func=AF.Ln, scale=1.0 / D, bias=col(sc, SC_EPS)),
                      reads=[("stt", c), "sc_eps"], writes=[("stt", 4 + c)])
                S.add("act", lambda e, c=c: e.activation(out=col(stt, 8 + c), in_=col(stt, 4 + c), func=AF.Exp, scale=-0.5),
                      reads=[("stt", 4 + c)], writes=[("stt", 8 + c)])
                s0 = 32 + 4 * (c % 2)
                stg = ar[:, s0:s0 + 4, :].rearrange("p a b -> p (a b)")
                S.add("dve", lambda e, c=c, stg=stg: e.tensor_scalar(out=stg, in0=hh[:, c, :], scalar1=col(stt, 8 + c), scalar2=None, op0=ALU.mult),
                      reads=HK(c) + [("stt", 8 + c)], writes=[SL(s0 + q) for q in range(4)])
                for G in range(4):
                    bk = nb()
                    pv = ps[:, bk, :].bitcast(BF16)
                    for q in range(4):
                        kc = 4 * G + q
                        S.add("pe", lambda e, pv=pv, q=q, kc=kc, stg=stg: e.transpose(out=pv[:, q * 128:(q + 1) * 128], in_=stg[:, kc * 128:(kc + 1) * 128],
                                                                                     identity=cm[:, IDENT, :]),
                              reads=[SL(s0 + kc // 4), "cm"], writes=[PS(bk)])
                    gb = pp[:, gcol0 + 4 * G:gcol0 + 4 * G + 4].unsqueeze(2).to_broadcast([128, 4, 128])
                    S.add("dve", lambda e, pv=pv, G=G, c=c, gb=gb: e.tensor_tensor(
                        out=ar[:, 4 * G:4 * G + 4, c * 128:(c + 1) * 128],
                        in0=pv[:, 0:512].rearrange("p (a b) -> p a b", a=4), in1=gb, op=ALU.mult),
                          reads=[PS(bk), "pp"], writes=[SL(4 * G + q) for q in range(4)])

        def dump(name, ap, shape, reads):
            d = nc.dram_tensor("dbg_" + name, list(shape), ap.dtype, kind="ExternalOutput").ap()
            dbg_d[name] = d
            S.add("sp", lambda e: e.dma_start(out=d, in_=ap), reads=reads, dma_key="dbg_" + name)

        ALLSL = [SL(i) for i in range(48)]

        for ti in range(ntiles if stop_after != "setup" else 0):
            tok0 = ti * T
            for c in range(4):
                S.add("sp", lambda e, c=c, tok0=tok0: e.dma_start(out=hh[:, c, :], in_=x_d[tok0 + c * 128:tok0 + (c + 1) * 128, :]),
                      writes=HK(c), dma_key=("x", c))
            S.add("sp", lambda e, tok0=tok0: e.dma_start(out=cs[:, 0, :], in_=cos_d[:, tok0:tok0 + T]), writes=["cos"], dma_key="cos")
            S.add("sp", lambda e, tok0=tok0: e.dma_start(out=cs[:, 1, :], in_=sin_d[:, tok0:tok0 + T]), writes=["sin"], dma_key="sin")
            norm_to_T(PP_N1G, "n1")
            if stop_after == "norm1":
                break

            pend = []

            def qk_post2(bz, zsq, zb, is_q, hd):
                bs_ = nb()
                br_ = nb()
                S.add("pe", lambda e: e.matmul(ps[:, bs_, :], lhsT=cm[:, BONES, :], rhs=tb[:, zsq, :], start=True, stop=True),
                      reads=["cm", ("tb", zsq)], writes=[PS(bs_)])
                S.add("pe", lambda e: e.matmul(ps[:, br_, :], lhsT=cm[:, ROT, :], rhs=tb[:, zb, :], start=True, stop=True),
                      reads=["cm", ("tb", zb)], writes=[PS(br_)])
                rs = ntf()
                t1 = ntf()
                t2 = ntf()
                S.add("act", lambda e: e.activation(out=tf[:, rs, :], in_=ps[:, bs_, :], func=AF.Ln, scale=1.0 / 64, bias=col(sc, SC_EPS)),
                      reads=[PS(bs_), "sc_eps"], writes=[("tf", rs)])
                S.add("act", lambda e: e.activation(out=tf[:, rs, :], in_=tf[:, rs, :], func=AF.Exp, scale=-0.5),
                      reads=[("tf", rs)], writes=[("tf", rs)])
                ga = SC_GQA if is_q else SC_GKA
                S.add("dve", lambda e: e.scalar_tensor_tensor(out=tf[:, t1, :], in0=ps[:, bz, :], scalar=col(sc, ga), in1=cs[:, 0, :],
                                                              op0=ALU.mult, op1=ALU.mult),
                      reads=[PS(bz), "cos"] + SCALL, writes=[("tf", t1)])
                S.add("dve", lambda e: e.scalar_tensor_tensor(out=tf[:, t2, :], in0=ps[:, br_, :], scalar=col(sc, ga + 1), in1=cs[:, 1, :],
                                                              op0=ALU.mult, op1=ALU.mult),
                      reads=[PS(br_), "sin"] + SCALL, writes=[("tf", t2)])
                S.add("dve", lambda e: e.tensor_tensor(out=tf[:, t1, :], in0=tf[:, t1, :], in1=tf[:, t2, :], op=ALU.add),
                      reads=[("tf", t1), ("tf", t2)], writes=[("tf", t1)])
                if is_q:
                    dst = ar[:, 16 + hd, :]
                    wk = [SL(16 + hd)]
                else:
                    dst = kT[:, hd, tok0:tok0 + T]
                    wk = [("kT", hd, ti)]
                S.add("dve", lambda e: e.tensor_tensor(out=dst, in0=tf[:, t1, :], in1=tf[:, rs, :], op=ALU.mult),
                      reads=[("tf", t1), ("tf", rs)], writes=wk)

            for is_q, nm in ((True, "q"), (False, "k")):
                for b in range(4):
                    n, slot, wv = wacq(f"{nm}{b}")
                    for half in range(2):
                        hd = 2 * b + half
                        bz = nb()
                        for kc in range(16):
                            S.add("pe", lambda e, bz=bz, kc=kc, wv=wv, half=half: e.matmul(
                                ps[:, bz, :], lhsT=wv[:, kc, half * 128:(half + 1) * 128], rhs=ar[:, kc, :], start=(kc == 0), stop=(kc == 15)),
                                  reads=[("w", slot), SL(kc)], writes=[PS(bz)])
                        zsq = ntb()
                        zb = ntb()
                        S.add("act", lambda e, bz=bz, zsq=zsq: e.activation(out=tb[:, zsq, :], in_=ps[:, bz, :], func=AF.Square),
                              reads=[PS(bz)], writes=[("tb", zsq)])
                        S.add("act", lambda e, bz=bz, zb=zb: e.activation(out=tb[:, zb, :], in_=ps[:, bz, :], func=AF.Copy),
                              reads=[PS(bz)], writes=[("tb", zb)])
                        for f in pend:
                            f()
                        pend = [lambda bz=bz, zsq=zsq, zb=zb, is_q=is_q, hd=hd: qk_post2(bz, zsq, zb, is_q, hd)]
                    wrel(n)
            for b in range(4):
                n, slot, wv = wacq(f"v{b}")
                for c in range(4):
                    bk = nb()
                    for kc in range(16):
                        S.add("pe", lambda e, bk=bk, kc=kc, wv=wv, c=c: e.matmul(
                            ps[:, bk, 0:256], lhsT=ar[:, kc, c * 128:(c + 1) * 128], rhs=wv[:, kc, :], start=(kc == 0), stop=(kc == 15)),
                              reads=[("w", slot), SL(kc)], writes=[PS(bk)])
                    if b == 0 and c == 0:
                        for f in pend:
                            f()
                        pend = []
                    dst = vv[:, 4 * ti + c, b * 256:(b + 1) * 256]
                    if c % 2 == 0:
                        S.add("act", lambda e, bk=bk, dst=dst: e.activation(out=dst, in_=ps[:, bk, 0:256], func=AF.Copy),
                              reads=[PS(bk)], writes=[("v", 4 * ti + c)])
                    else:
                        S.add("dve", lambda e, bk=bk, dst=dst: e.tensor_copy(out=dst, in_=ps[:, bk, 0:256]),
                              reads=[PS(bk)], writes=[("v", 4 * ti + c)])
                wrel(n)
            for b in range(4):
                n, slot, wv = wacq(f"u{b}")
                for half in range(2):
                    g = 2 * b + half
                    bk = nb()
                    for kc in range(16):
                        S.add("pe", lambda e, bk=bk, kc=kc, wv=wv, half=half: e.matmul(
                            ps[:, bk, :], lhsT=wv[:, kc, half * 128:(half + 1) * 128], rhs=ar[:, kc, :], start=(kc == 0), stop=(kc == 15)),
                              reads=[("w", slot), SL(kc)], writes=[PS(bk)])
                    S.add("act", lambda e, bk=bk, g=g: e.activation(out=ar[:, 24 + g, :], in_=ps[:, bk, :], func=AF.Gelu),
                          reads=[PS(bk)], writes=[SL(24 + g)])
                wrel(n)
            S.add("dve", lambda e: e.memset(stt[:, 16:48], 0.0), writes=["stt_sv"])

            def vn_ap(c, c0, c1):
                return ar[:, 32 + 2 * c:34 + 2 * c, :].rearrange("p a b -> p (a b)")[:, c0:c1]

            for b in range(4):
                n, slot, wv = wacq(f"sv{b}")
                for c in range(4):
                    bk = nb()
                    for kc in range(16):
                        S.add("pe", lambda e, bk=bk, kc=kc, wv=wv, c=c: e.matmul(
                            ps[:, bk, 0:256], lhsT=ar[:, kc, c * 128:(c + 1) * 128], rhs=wv[:, kc, :], start=(kc == 0), stop=(kc == 15)),
                              reads=[("w", slot), SL(kc)], writes=[PS(bk)])
                    dst = vn_ap(c, b * 256, (b + 1) * 256)
                    S.add("act", lambda e, bk=bk, dst=dst, c=c, b=b: e.activation(out=dst, in_=ps[:, bk, 0:256], func=AF.Gelu,
                                                                              accum_out=col(stt, 16 + 4 * c + b)),
                          reads=[PS(bk), "stt_sv"], writes=[SL(32 + 2 * c), SL(33 + 2 * c), ("sts", c, b)])
                    jk = ntb()
                    S.add("act", lambda e, dst=dst, jk=jk, c=c, b=b: e.activation(out=tb[:, jk, 0:256], in_=dst, func=AF.Square,
                                                                              accum_out=col(stt, 32 + 4 * c + b)),
                          reads=[SL(32 + 2 * c), SL(33 + 2 * c), "stt_sv"], writes=[("tb", jk), ("stq", c, b)])
                wrel(n)
            def sv_stats_a():
                rd = [("sts", c, b) for c in range(4) for b in range(4)] + [("stq", c, b) for c in range(4) for b in range(4)]
                X = mybir.AxisListType.X
                S.add("dve", lambda e: e.reduce_sum(out=stt[:, 48:52], in_=stt[:, 16:32].rearrange("p (c b) -> p c b", b=4), axis=X),
                      reads=rd, writes=["svA"])
                S.add("dve", lambda e: e.reduce_sum(out=stt[:, 52:56], in_=stt[:, 32:48].rearrange("p (c b) -> p c b", b=4), axis=X),
                      reads=rd + ["svA"], writes=["svA"])
                S.add("dve", lambda e: e.tensor_scalar(out=stt[:, 48:52], in0=stt[:, 48:52], scalar1=1.0 / 1024, scalar2=None, op0=ALU.mult),
                      reads=["svA"], writes=["svA"])
                S.add("dve", lambda e: e.tensor_tensor(out=stt[:, 56:60], in0=stt[:, 48:52], in1=stt[:, 48:52], op=ALU.mult),
                      reads=["svA"], writes=["svA"])
                S.add("dve", lambda e: e.scalar_tensor_tensor(out=stt[:, 56:60], in0=stt[:, 52:56], scalar=1.0 / 1024, in1=stt[:, 56:60],
                                                              op0=ALU.mult, op1=ALU.subtract),
                      reads=["svA"], writes=["svA"])

            def sv_stats_b():
                S.add("act", lambda e: e.activation(out=stt[:, 60:64], in_=stt[:, 56:60], func=AF.Ln, bias=col(sc, SC_EPS)),
                      reads=["svA", "sc_eps"], writes=["svB"])
                S.add("act", lambda e: e.activation(out=stt[:, 60:64], in_=stt[:, 60:64], func=AF.Exp, scale=-0.5),
                      reads=["svB"], writes=["svB"])
                for c in range(4):
                    vfull = vn_ap(c, 0, 1024)
                    S.add("dve", lambda e, c=c, vfull=vfull: e.tensor_scalar(out=vfull, in0=vfull, scalar1=col(stt, 48 + c), scalar2=col(stt, 60 + c),
                                                                             op0=ALU.subtract, op1=ALU.mult),
                          reads=[SL(32 + 2 * c), SL(33 + 2 * c), "svA", "svB"], writes=[SL(32 + 2 * c), SL(33 + 2 * c)])

            sv_stats_a()
            if stop_after == "proj":
                sv_stats_b()
                break

            nj = 4 * (ti + 1)
            late = []
            for hd in range(8):
                pav = []
                qz0 = 40 + 2 * (hd % 4)
                qz1 = qz0 + 1
                S.add("dve", lambda e, hd=hd, qz0=qz0: e.tensor_scalar(out=ar[:, qz0, :], in0=ar[:, 16 + hd, :], scalar1=col(sc, SC_M0), scalar2=None, op0=ALU.mult),
                      reads=[SL(16 + hd), "sc_m"], writes=[SL(qz0)])
                S.add("dve", lambda e, hd=hd, qz1=qz1: e.tensor_scalar(out=ar[:, qz1, :], in0=ar[:, 16 + hd, :], scalar1=col(sc, SC_M1), scalar2=None, op0=ALU.mult),
                      reads=[SL(16 + hd), "sc_m"], writes=[SL(qz1)])
                for j in range(nj):
                    r = j - 4 * ti
                    c0 = 128 * r if r > 0 else 0
                    pr = cnt["pair"] % 2
                    cnt["pair"] += 1
                    sb0, sb1 = 2 * pr, 2 * pr + 1
                    ksl = slice(j * 128, (j + 1) * 128)
                    kkey = ("kT", hd, j // 4)
                    S.add("pe", lambda e, sb0=sb0, c0=c0, ksl=ksl, hd=hd, qz0=qz0: e.matmul(
                        ps[:, sb0, c0:512], lhsT=kT[:, hd, ksl], rhs=ar[:, qz0, c0:512], start=True, stop=True),
                          reads=[kkey, SL(qz0)], writes=[PS(sb0)])
                    S.add("pe", lambda e, sb1=sb1, c0=c0, ksl=ksl, hd=hd, qz1=qz1: e.matmul(
                        ps[:, sb1, c0:512], lhsT=kT[:, hd, ksl], rhs=ar[:, qz1, c0:512], start=True, stop=True),
                          reads=[kkey, SL(qz1)], writes=[PS(sb1)])
                    p0 = ntb()
                    p1 = ntb()
                    S.add("act", lambda e, sb0=sb0, p0=p0, c0=c0: e.activation(out=tb[:, p0, c0:512], in_=ps[:, sb0, c0:512], func=AF.Exp),
                          reads=[PS(sb0)], writes=[("tb", p0)])
                    S.add("act", lambda e, sb1=sb1, p1=p1, c0=c0: e.activation(out=tb[:, p1, c0:512], in_=ps[:, sb1, c0:512], func=AF.Exp),
                          reads=[PS(sb1)], writes=[("tb", p1)])
                    if r >= 0:
                        for pq in (p0, p1):
                            S.add("dve", lambda e, pq=pq, c0=c0: e.tensor_tensor(out=tb[:, pq, c0:c0 + 128], in0=tb[:, pq, c0:c0 + 128],
                                                                                 in1=cm[:, TRI, :], op=ALU.mult),
                                  reads=[("tb", pq), "cm"], writes=[("tb", pq)])
                    for f in pav:
                        f()

                    def av(j=j, c0=c0, p0=p0, p1=p1, hd=hd, nj=nj):
                        vsl = vv[:, j, hd * 128:(hd + 1) * 128]
                        for (bo, bsum, pq) in ((4, 6, p0), (5, 7, p1)):
                            S.add("pe", lambda e, bo=bo, pq=pq, vsl=vsl: e.matmul(
                                ps[:, bo, c0:512], lhsT=vsl, rhs=tb[:, pq, c0:512], start=(j == 0), stop=(j == nj - 1)),
                                  reads=[("v", j), ("tb", pq)], writes=[PS(bo)])
                            S.add("pe", lambda e, bsum=bsum, pq=pq: e.matmul(
                                ps[:, bsum, c0:512], lhsT=cm[:, ONES, :], rhs=tb[:, pq, c0:512], start=(j == 0), stop=(j == nj - 1)),
                                  reads=["cm", ("tb", pq)], writes=[PS(bsum)])
                    pav = [av]
                    if j == min(6, nj - 1):
                        for f in late:
                            f()
                        late = []
                for f in pav:
                    f()
                r0 = ntf()
                rb = ntf()
                rc = ntf()
                r1 = ntf()
                S.add("act", lambda e, r0=r0: e.activation(out=tf[:, r0, :], in_=ps[:, 4, :], func=AF.Copy), reads=[PS(4)], writes=[("tf", r0)])
                S.add("dve", lambda e, rb=rb: e.tensor_copy(out=tf[:, rb, :], in_=ps[:, 5, :]), reads=[PS(5)], writes=[("tf", rb)])
                S.add("act", lambda e, rc=rc: e.activation(out=tf[:, rc, :], in_=ps[:, 6, :], func=AF.Ln), reads=[PS(6)], writes=[("tf", rc)])
                S.add("dve", lambda e, r1=r1: e.tensor_copy(out=tf[:, r1, :], in_=ps[:, 7, :]), reads=[PS(7)], writes=[("tf", r1)])
                S.add("act", lambda e, rc=rc: e.activation(out=tf[:, rc, :], in_=tf[:, rc, :], func=AF.Exp, scale=-1.0),
                      reads=[("tf", rc)], writes=[("tf", rc)])
                S.add("act", lambda e, r1=r1: e.activation(out=tf[:, r1, :], in_=tf[:, r1, :], func=AF.Ln), reads=[("tf", r1)], writes=[("tf", r1)])
                S.add("act", lambda e, r1=r1: e.activation(out=tf[:, r1, :], in_=tf[:, r1, :], func=AF.Exp, scale=-1.0),
                      reads=[("tf", r1)], writes=[("tf", r1)])
                S.add("dve", lambda e, r0=r0, rc=rc: e.tensor_tensor(out=tf[:, r0, :], in0=tf[:, r0, :], in1=tf[:, rc, :], op=ALU.mult),
                      reads=[("tf", r0), ("tf", rc)], writes=[("tf", r0)])
                S.add("dve", lambda e, rb=rb, r1=r1: e.tensor_tensor(out=tf[:, rb, :], in0=tf[:, rb, :], in1=tf[:, r1, :], op=ALU.mult),
                      reads=[("tf", rb), ("tf", r1)], writes=[("tf", rb)])
                S.add("dve", lambda e, r0=r0, rb=rb: e.scalar_tensor_tensor(out=tf[:, r0, :], in0=tf[:, rb, :], scalar=col(sc, SC_NLAM), in1=tf[:, r0, :],
                                                                            op0=ALU.mult, op1=ALU.add),
                      reads=[("tf", r0), ("tf", rb)] + SCALL, writes=[("tf", r0)])
                if debug and hd == 0 and ti == 0:
                    dump("o0", tf[:, r0, :], [128, 512], [("tf", r0)])
                S.add("act", lambda e, r0=r0: e.activation(out=osqb[:], in_=tf[:, r0, :], func=AF.Square),
                      reads=[("tf", r0)], writes=["osq"])

                def subln(r0=r0, r1=r1, hd=hd):
                    pr = cnt["pair"] % 2
                    cnt["pair"] += 1
                    bk = 2 * pr
                    S.add("pe", lambda e: e.matmul(ps[:, bk, :], lhsT=cm[:, ONES, :], rhs=osqb[:], start=True, stop=True),
                          reads=["cm", "osq"], writes=[PS(bk)])
                    S.add("act", lambda e: e.activation(out=tf[:, r1, :], in_=ps[:, bk, :], func=AF.Ln, scale=1.0 / 128, bias=col(sc, SC_EPS)),
                          reads=[PS(bk), "sc_eps"], writes=[("tf", r1)])
                    S.add("act", lambda e: e.activation(out=tf[:, r1, :], in_=tf[:, r1, :], func=AF.Exp, scale=-0.5),
                          reads=[("tf", r1)], writes=[("tf", r1)])
                    S.add("dve", lambda e: e.scalar_tensor_tensor(out=ar[:, 16 + hd, :], in0=tf[:, r0, :], scalar=col(sc, SC_SUBG), in1=tf[:, r1, :],
                                                                  op0=ALU.mult, op1=ALU.mult),
                          reads=[("tf", r0), ("tf", r1)] + SCALL, writes=[SL(16 + hd)])
                late = [subln]
                if hd == 0:
                    sv_stats_b()
            for g in range(8):
                bk = nb()
                for c in range(4):
                    S.add("pe", lambda e, bk=bk, c=c, g=g: e.matmul(
                        ps[:, bk, c * 128:(c + 1) * 128], lhsT=vn_ap(c, g * 128, (g + 1) * 128), rhs=wTm[:, g, :], start=True, stop=True),
                          reads=[SL(32 + 2 * c), SL(33 + 2 * c), "wTm"], writes=[PS(bk)])
                for c in range(4):
                    S.add("dve", lambda e, bk=bk, c=c, g=g: e.scalar_tensor_tensor(
                        out=ps[:, bk, c * 128:(c + 1) * 128], in0=ps[:, bk, c * 128:(c + 1) * 128], scalar=col(pp, PP_LNG + g),
                        in1=b2[:, g, :], op0=ALU.mult, op1=ALU.add),
                          reads=[PS(bk), "pp", "b2"], writes=[PS(bk)])
                S.add("dve", lambda e, g=g, bk=bk: e.tensor_tensor(out=ar[:, 24 + g, :], in0=ps[:, bk, :], in1=ar[:, 24 + g, :], op=ALU.mult),
                      reads=[PS(bk), SL(24 + g)], writes=[SL(24 + g)])
                if g == 5:
                    for f in late:
                        f()
                    late = []
            if stop_after == "mix":
                break

            for jp in range(8):
                res = {}
                for nm, kd, src0 in (("ga", 16, 0), ("gs", 16, 0), ("att", 8, 16), ("sgu", 8, 24)):
                    n, slot, wv = wacq(f"{nm}{jp}")
                    for half in range(2):
                        bk = nb()
                        for kc in range(kd):
                            S.add("pe", lambda e, bk=bk, kc=kc, wv=wv, half=half, kd=kd, src0=src0: e.matmul(
                                ps[:, bk, :], lhsT=wv[:, kc, half * 128:(half + 1) * 128], rhs=ar[:, src0 + kc, :], start=(kc == 0), stop=(kc == kd - 1)),
                                  reads=[("w", slot), SL(src0 + kc)], writes=[PS(bk)])
                        res[(nm, half)] = bk
                        if nm in ("ga", "gs"):
                            j = 2 * jp + half
                            bcol = j if nm == "ga" else 16 + j
                            t = ntf()
                            S.add("act", lambda e, bk=bk, t=t, bcol=bcol: e.activation(out=tf[:, t, :], in_=ps[:, bk, :], func=AF.Tanh, scale=0.5,
                                                                                   bias=col(hb, bcol)),
                                  reads=[PS(bk), "hb"], writes=[("tf", t)])
                            res[(nm + "t", half)] = t
                    wrel(n)
                for half in range(2):
                    j = 2 * jp + half
                    ta, ts_ = res[("gat", half)], res[("gst", half)]
                    ba, bs_ = res[("att", half)], res[("sgu", half)]
                    S.add("dve", lambda e, ta=ta, ba=ba: e.scalar_tensor_tensor(out=tf[:, ta, :], in0=tf[:, ta, :], scalar=1.0, in1=ps[:, ba, :],
                                                                                op0=ALU.add, op1=ALU.mult),
                          reads=[("tf", ta), PS(ba)], writes=[("tf", ta)])
                    S.add("dve", lambda e, ts_=ts_, bs_=bs_: e.scalar_tensor_tensor(out=tf[:, ts_, :], in0=tf[:, ts_, :], scalar=1.0, in1=ps[:, bs_, :],
                                                                                    op0=ALU.add, op1=ALU.mult),
                          reads=[("tf", ts_), PS(bs_)], writes=[("tf", ts_)])
                    S.add("dve", lambda e, ta=ta, ts_=ts_, j=j: e.tensor_tensor(out=ar[:, 32 + j, :], in0=tf[:, ta, :], in1=tf[:, ts_, :], op=ALU.add),
                          reads=[("tf", ta), ("tf", ts_)], writes=[SL(32 + j)])
            for b in range(8):
                n, slot, wv = wacq(f"wo{b}")
                for c in range(4):
                    bk = nb()
                    for kc in range(16):
                        S.add("pe", lambda e, bk=bk, kc=kc, wv=wv, c=c: e.matmul(
                            ps[:, bk, 0:256], lhsT=ar[:, 32 + kc, c * 128:(c + 1) * 128], rhs=wv[:, kc, :], start=(kc == 0), stop=(kc == 15)),
                              reads=[("w", slot), SL(32 + kc)], writes=[PS(bk)])
                    hs = hh[:, c, b * 256:(b + 1) * 256]
                    S.add("dve", lambda e, bk=bk, hs=hs: e.scalar_tensor_tensor(out=hs, in0=ps[:, bk, 0:256], scalar=0.5, in1=hs, op0=ALU.mult, op1=ALU.add),
                          reads=[PS(bk), ("h", c, b // 2)], writes=[("h", c, b // 2)])
                wrel(n)
            if stop_after == "mixer":
                break
            norm_to_T(PP_N2G, "n2")
            for g in range(3):
                nch = 16 if g < 2 else 12
                for pr in range(nch // 2):
                    j0 = 16 * g + 2 * pr
                    ng, sg, wg = wacq(f"upg{j0}")
                    nv, sv_, wvv = wacq(f"upv{j0}")
                    for half in range(2):
                        jj = j0 + half
                        tt = {}
                        for kind, slot, wv, chn in (("g", sg, wg, jj), ("v", sv_, wvv, NHC + jj)):
                            bk = nb()
                            for kc in range(16):
                                S.add("pe", lambda e, bk=bk, kc=kc, wv=wv, half=half: e.matmul(
                                    ps[:, bk, :], lhsT=wv[:, kc, half * 128:(half + 1) * 128], rhs=ar[:, kc, :], start=(kc == 0), stop=(kc == 15)),
                                      reads=[("w", slot), SL(kc)], writes=[PS(bk)])
                            t0 = ntf()
                            w0 = col(pp, PP_CW + 3 * chn + 0)
                            w1 = col(pp, PP_CW + 3 * chn + 1)
                            w2 = col(pp, PP_CW + 3 * chn + 2)
                            cb_ = col(pp, PP_CB + chn)
                            ck = ("carry", chn)
                            S.add("act", lambda e, bk=bk, t0=t0, w2=w2, cb_=cb_: e.activation(out=tf[:, t0, :], in_=ps[:, bk, :], func=AF.Identity,
                                                                                          scale=w2, bias=cb_),
                                  reads=[PS(bk), "pp"], writes=[("tf", t0)])
                            S.add("dve", lambda e, bk=bk, t0=t0, w1=w1: e.scalar_tensor_tensor(
                                out=tf[:, t0, 1:512], in0=ps[:, bk, 0:511], scalar=w1, in1=tf[:, t0, 1:512], op0=ALU.mult, op1=ALU.add),
                                  reads=[PS(bk), ("tf", t0), "pp"], writes=[("tf", t0)])
                            S.add("dve", lambda e, bk=bk, t0=t0, w0=w0: e.scalar_tensor_tensor(
                                out=tf[:, t0, 2:512], in0=ps[:, bk, 0:510], scalar=w0, in1=tf[:, t0, 2:512], op0=ALU.mult, op1=ALU.add),
                                  reads=[PS(bk), ("tf", t0), "pp"], writes=[("tf", t0)])
                            S.add("dve", lambda e, t0=t0, w0=w0, chn=chn: e.scalar_tensor_tensor(
                                out=tf[:, t0, 0:2], in0=carry[:, chn, :], scalar=w0, in1=tf[:, t0, 0:2], op0=ALU.mult, op1=ALU.add),
                                  reads=[ck, ("tf", t0), "pp", "carry"], writes=[("tf", t0)])
                            S.add("dve", lambda e, t0=t0, w1=w1, chn=chn: e.scalar_tensor_tensor(
                                out=tf[:, t0, 0:1], in0=carry[:, chn, 1:2], scalar=w1, in1=tf[:, t0, 0:1], op0=ALU.mult, op1=ALU.add),
                                  reads=[ck, ("tf", t0), "pp", "carry"], writes=[("tf", t0)])
                            S.add("dve", lambda e, bk=bk, chn=chn: e.tensor_copy(out=carry[:, chn, :], in_=ps[:, bk, 510:512]),
                                  reads=[PS(bk), "carry"], writes=[ck])
                            tt[kind] = t0
                        gl = ntb()
                        S.add("act", lambda e, gl=gl, tg=tt["g"]: e.activation(out=tb[:, gl, :], in_=tf[:, tg, :], func=AF.Gelu),
                              reads=[("tf", tt["g"])], writes=[("tb", gl)])
                        hsl = 16 + (jj - 16 * g)
                        S.add("dve", lambda e, gl=gl, tv=tt["v"], hsl=hsl: e.tensor_tensor(out=ar[:, hsl, :], in0=tb[:, gl, :], in1=tf[:, tv, :], op=ALU.mult),
                              reads=[("tb", gl), ("tf", tt["v"])], writes=[SL(hsl)])
                    wrel(ng)
                    wrel(nv)
                for cb in range(4):
                    nhf = nch // 8 + (1 if nch % 8 else 0)
                    banks = [nb() for _ in range(4)]
                    for hf in range(nhf):
                        n, slot, wv = wacq(f"dn{g}_{cb}_{hf}")
                        nk = min(8, nch - 8 * hf)
                        for c in range(4):
                            for k in range(nk):
                                kl = 8 * hf + k
                                S.add("pe", lambda e, bk=banks[c], k=k, kl=kl, wv=wv, c=c, nch=nch: e.matmul(
                                    ps[:, bk, :], lhsT=ar[:, 16 + kl, c * 128:(c + 1) * 128], rhs=wv[:, k, :], start=(kl == 0), stop=(kl == nch - 1)),
                                      reads=[("w", slot), SL(16 + kl)], writes=[PS(banks[c])])
                        wrel(n)
                    for c in range(4):
                        hs = hh[:, c, cb * 512:(cb + 1) * 512]
                        if g == 2 and cb == 3:
                            stg_ = ntf()
                            S.add("dve", lambda e, bk=banks[c], hs=hs, stg_=stg_: e.tensor_tensor(out=tf[:, stg_, :], in0=ps[:, bk, :], in1=hs, op=ALU.add),
                                  reads=[PS(banks[c]), ("h", c, cb)], writes=[("tf", stg_)])
                            od = out_d[tok0 + c * 128:tok0 + (c + 1) * 128, cb * 512:(cb + 1) * 512]
                            S.add("sp", lambda e, od=od, stg_=stg_: e.dma_start(out=od, in_=tf[:, stg_, :]),
                                  reads=[("tf", stg_)], dma_key=("o", c, cb))
                            continue
                        S.add("dve", lambda e, bk=banks[c], hs=hs: e.tensor_tensor(out=hs, in0=ps[:, bk, :], in1=hs, op=ALU.add),
                              reads=[PS(banks[c]), ("h", c, cb)], writes=[("h", c, cb)])
                        if g == 2:
                            od = out_d[tok0 + c * 128:tok0 + (c + 1) * 128, cb * 512:(cb + 1) * 512]
                            S.add("sp", lambda e, od=od, hs=hs: e.dma_start(out=od, in_=hs),
                                  reads=[("h", c, cb)], dma_key=("o", c, cb))

        if debug:
            dump("kT", kT[:], [128, 8, SEQ], [("kT", h_, t_) for h_ in range(8) for t_ in range(ntiles)])
            dump("vv", vv[:], [128, 16, 1024], [("v", j) for j in range(16)])
            dump("ar", ar[:], [128, 48, 512], ALLSL)
            dump("hh", hh[:], [128, 4, D], [k for c in range(4) for k in HK(c)])
        S.emit(nc)
    return nc, S


def _consts():
    p = np.arange(128)
    ident = np.eye(128, dtype=np.float32)
    rot = np.zeros((128, 128), np.float32)
    for m in range(128):
        d = m % 64
        if d < 32:
            rot[m + 32, m] = -1.0
        else:
            rot[m - 32, m] = 1.0
    bones = (p[:, None] // 64 == p[None, :] // 64).astype(np.float32)
    tri = (p[None, :] >= p[:, None]).astype(np.float32)
    ones = np.ones((128, 128), np.float32)
    cmf = np.stack([ident, rot, bones, tri, ones], axis=1).reshape(128, 5 * 128)
    inv = np.exp(-math.log(10000.0) * np.arange(0, 64, 2, dtype=np.float32) / 64).astype(np.float32)
    ang = np.arange(SEQ, dtype=np.float32)[:, None] * inv[None, :]
    idx = (p % 64) % 32
    cosT = np.cos(ang).astype(np.float32).T[idx]
    sinT = np.sin(ang).astype(np.float32).T[idx]
    return np.ascontiguousarray(cmf), np.ascontiguousarray(cosT), np.ascontiguousarray(sinT)


def _layout_params(inp):
    f = lambda a: np.asarray(a, dtype=np.float32)
    pp = np.zeros((128, PP_END), np.float32)
    pp[:, PP_N1G:PP_N1G + 16] = f(inp["norm1_g"])[0].reshape(16, 128).T
    pp[:, PP_N2G:PP_N2G + 16] = f(inp["norm2_g"])[0].reshape(16, 128).T
    pp[:, PP_BG:PP_BG + 32] = f(inp["b_gate"])[0].reshape(32, 128).T
    d = np.arange(128) % 64
    qg = f(inp["q_norm_g"])[0]
    kg = f(inp["k_norm_g"])[0]
    pp[:, PP_GQA] = qg[d]
    pp[:, PP_GQB] = qg[(d + 32) % 64]
    pp[:, PP_GKA] = kg[d]
    pp[:, PP_GKB] = kg[(d + 32) % 64]
    pp[:, PP_SUBG] = f(inp["subln_g"])[0]
    pp[:, PP_LNG:PP_LNG + 8] = f(inp["sgu_norm_g"])[0].reshape(8, 128).T
    pp[:, PP_LNB:PP_LNB + 8] = f(inp["sgu_norm_b"])[0].reshape(8, 128).T
    cw = f(inp["conv_w"])[0]
    pp[:, PP_CW:PP_CW + 264] = cw.reshape(3, 88, 128).transpose(2, 1, 0).reshape(128, 264)
    pp[:, PP_CB:PP_CB + 88] = f(inp["conv_b"])[0].reshape(88, 128).T
    lamb = np.concatenate([f(inp["lambda_q1"])[0], f(inp["lambda_k1"])[0], f(inp["lambda_q2"])[0], f(inp["lambda_k2"])[0]])
    lamb = np.ascontiguousarray(np.broadcast_to(lamb[None, :], (128, 256)))
    wT = np.ascontiguousarray(f(inp["sgu_w"])[0].transpose(2, 0, 1).reshape(128, 1024))
    bs = np.ascontiguousarray(np.broadcast_to(f(inp["sgu_b"])[0].reshape(1, 1024), (128, 1024)))
    return pp, lamb, wT, bs


_CACHE = {}


def kernel(**inputs):
    f = lambda a: np.ascontiguousarray(np.asarray(a, dtype=np.float32))
    if "nc" not in _CACHE:
        _CACHE["nc"] = build()[0]
    nc = _CACHE["nc"]
    cmf, cosT, sinT = _consts()
    pp, lamb, wT, bs = _layout_params(inputs)
    x = f(inputs["x"])
    shared = {
        "w_in": f(inputs["w_in"])[0], "w_att_out": f(inputs["w_att_out"])[0], "w_sgu_out": f(inputs["w_sgu_out"])[0],
        "w_out": f(inputs["w_out"])[0], "w_up": f(inputs["w_up"])[0], "w_down": f(inputs["w_down"])[0],
        "pp": pp, "lamb": lamb, "cmf": cmf, "sgu_wT": wT, "sgu_bs": bs, "cosT": cosT, "sinT": sinT,
    }
    in_maps = [dict(shared, x=x[b]) for b in range(8)]
    res = run_bass_kernel_spmd(nc, in_maps, core_ids=list(range(8)))
    return np.stack([np.asarray(r["out"], dtype=np.float32) for r in res.results], axis=0)
```

```python
import contextlib
import math
import numpy as np
import concourse.bass as bass
import concourse.mybir as mybir
from concourse.bass_utils import run_bass_kernel_spmd

F32 = mybir.dt.float32
BF16 = mybir.dt.bfloat16
AF = mybir.ActivationFunctionType
ALU = mybir.AluOpType

D = 2048
SEQ = 2048
T = 512
NTILE = 4
INC = 9216
DFF = 5632
NHC = 44
EPS = 1e-6
NS = 4
NTF = 5
NTB = 6
COMPUTE = ("pe", "act", "dve", "pool")

PP_N1G, PP_N2G, PP_BG, PP_GQA, PP_GQB, PP_GKA, PP_GKB, PP_SUBG, PP_LNG, PP_LNB, PP_CW, PP_CB, PP_END = (
    0, 16, 32, 64, 65, 66, 67, 68, 69, 77, 85, 349, 437)


class Sched:
    def __init__(self):
        self.ops = []
        self.last_writer = {}
        self.readers = {}

    def add(self, eng, fn, reads=(), writes=(), dma_key=None):
        idx = len(self.ops)
        deps = set()
        psr = [b for b in reads if isinstance(b, tuple) and b[0] == "ps"]
        if psr:
            reads = [b for b in reads if b not in psr]
            writes = list(writes) + [b for b in psr if b not in writes]
        for b in reads:
            w = self.last_writer.get(b)
            if w is not None:
                deps.add(w)
        for b in writes:
            w = self.last_writer.get(b)
            if w is not None:
                deps.add(w)
            for r in self.readers.get(b, ()):
                deps.add(r)
        for b in reads:
            self.readers.setdefault(b, []).append(idx)
        for b in writes:
            self.last_writer[b] = idx
            self.readers[b] = []
        self.ops.append(dict(eng=eng, fn=fn, deps=deps, dma_key=dma_key, idx=idx))
        return idx

    def emit(self, nc):
        ops = self.ops
        needed = set()
        for o in ops:
            nd = set()
            best = {}
            for d in o["deps"]:
                p = ops[d]
                if p["dma_key"] is not None:
                    nd.add(d)
                    continue
                if p["eng"] == "pe" and o["eng"] == "pe" and o["dma_key"] is None:
                    continue
                if best.get(p["eng"], -1) < d:
                    best[p["eng"]] = d
            nd |= set(best.values())
            o["deps"] = nd
            needed |= nd
        eng_count = {e: 0 for e in COMPUTE}
        key_count = {}
        for o in ops:
            if o["dma_key"] is not None:
                k = o["dma_key"]
                key_count[k] = key_count.get(k, 0) + 16
                o["sem"] = ("dma", k)
                o["cnt"] = key_count[k]
            elif o["idx"] in needed:
                e = o["eng"]
                eng_count[e] += 1
                o["sem"] = ("eng", e)
                o["cnt"] = eng_count[e]
            else:
                o["sem"] = None
        sem_names = [("eng", e) for e in COMPUTE] + [("dma", k) for k in key_count]
        with contextlib.ExitStack() as st:
            sems = {}
            for n_, sn in enumerate(sem_names):
                sems[sn] = st.enter_context(nc.semaphore(f"sem{n_}"))
            block = st.enter_context(nc.Block())
            per_eng = {e: [o for o in ops if o["eng"] == e] for e in COMPUTE + ("sp",)}

            def run(engobj, ename):
                waited = {}
                for o in per_eng[ename]:
                    need = {}
                    for d in o["deps"]:
                        p = ops[d]
                        s = p["sem"]
                        if need.get(s, 0) < p["cnt"]:
                            need[s] = p["cnt"]
                    for s, c in need.items():
                        if waited.get(s, 0) >= c:
                            continue
                        engobj.wait_ge(sems[s], c)
                        waited[s] = c
                    ins = o["fn"](engobj)
                    if o["sem"] is not None:
                        ins.then_inc(sems[o["sem"]], 16 if o["sem"][0] == "dma" else 1)
                if ename == "sp":
                    for k, c in key_count.items():
                        engobj.wait_ge(sems[("dma", k)], c)

            @block.tensor
            def _(e):
                run(e, "pe")

            @block.scalar
            def _(e):
                run(e, "act")

            @block.vector
            def _(e):
                run(e, "dve")

            @block.gpsimd
            def _(e):
                run(e, "pool")

            @block.sync
            def _(e):
                run(e, "sp")


def build(ntiles=NTILE, stop_after=None, debug=False):
    nc = bass.Bass("TRN2", target_bir_lowering=False)

    def din(name, shape):
        return nc.dram_tensor(name, list(shape), F32, kind="ExternalInput").ap()

    x_d = din("x", [SEQ, D])
    w_in_d = din("w_in", [D, INC])
    w_att_d = din("w_att_out", [1024, D])
    w_sgu_d = din("w_sgu_out", [1024, D])
    w_out_d = din("w_out", [D, D])
    w_up_d = din("w_up", [D, 2 * DFF])
    w_down_d = din("w_down", [DFF, D])
    pp_d = din("pp", [128, PP_END])
    lamb_d = din("lamb", [128, 256])
    cmf_d = din("cmf", [128, 5 * 128])
    wT_d = din("sgu_wT", [128, 8 * 128])
    bs_d = din("sgu_bs", [128, 8 * 128])
    cos_d = din("cosT", [128, SEQ])
    sin_d = din("sinT", [128, SEQ])
    out_d = nc.dram_tensor("out", [SEQ, D], F32, kind="ExternalOutput").ap()
    dbg_d = {}

    S = Sched()
    st = contextlib.ExitStack()
    with st:
        def sb(name, shape, dt):
            return st.enter_context(nc.sbuf_tensor("sb_" + name, list(shape), dt))

        kT = sb("kT", [128, 8, SEQ], BF16)
        vv = sb("vv", [128, 16, 1024], BF16)
        ar = sb("ar", [128, 48, 512], BF16)
        hh = sb("hh", [128, 4, D], F32)
        wr = sb("wr", [128, NS, 4096], BF16)
        cm = sb("cm", [128, 5, 128], BF16)
        pp = sb("pp", [128, PP_END], F32)
        cs = sb("cs", [128, 2, 512], F32)
        wTm = sb("wTm", [128, 8, 128], BF16)
        b2 = sb("b2", [128, 8, 128], F32)
        hb = sb("hb", [128, 32], F32)
        sc = sb("sc", [128, 16], F32)
        stt = sb("stt", [128, 64], F32)
        carry = sb("carry", [128, 88, 2], F32)
        tf = sb("tf", [128, NTF, 512], F32)
        tb = sb("tb", [128, NTB, 512], BF16)
        osqb = sb("osqb", [128, 512], BF16)
        ps = st.enter_context(nc.psum_tensor("ps", [128, 8, 512], F32))

        IDENT, ROT, BONES, TRI, ONES = range(5)
        SC_GQA, SC_GQB, SC_GKA, SC_GKB, SC_SUBG, SC_NLAM, SC_EPS, SC_T0, SC_T1, SC_T2, SC_T3, SC_M0, SC_M1 = range(13)

        def col(t, k):
            return t[:, k:k + 1]

        cnt = dict(bank=0, tf=0, tb=0, pair=0)

        def nb():
            b = cnt["bank"] % 8
            cnt["bank"] += 1
            return b

        def ntf():
            i = cnt["tf"] % NTF
            cnt["tf"] += 1
            return i

        def ntb():
            i = cnt["tb"] % NTB
            cnt["tb"] += 1
            return i

        def PS(b):
            return ("ps", b)

        def SL(i):
            return ("sl", i)

        def HK(c):
            return [("h", c, q) for q in range(4)]

        plan = []

        def plan_tile():
            for nm, base in (("q", 0), ("k", 1024), ("v", 2048), ("u", 3072), ("sv", 4096)):
                for b in range(4):
                    c0 = base + 256 * b
                    plan.append((f"{nm}{b}", w_in_d[:, c0:c0 + 256].rearrange("(k p) n -> p k n", p=128), 16, 256))
            for jp in range(8):
                c0 = 5120 + 256 * jp
                plan.append((f"ga{jp}", w_in_d[:, c0:c0 + 256].rearrange("(k p) n -> p k n", p=128), 16, 256))
                c0 = 7168 + 256 * jp
                plan.append((f"gs{jp}", w_in_d[:, c0:c0 + 256].rearrange("(k p) n -> p k n", p=128), 16, 256))
                plan.append((f"att{jp}", w_att_d[:, 256 * jp:256 * jp + 256].rearrange("(k p) n -> p k n", p=128), 8, 256))
                plan.append((f"sgu{jp}", w_sgu_d[:, 256 * jp:256 * jp + 256].rearrange("(k p) n -> p k n", p=128), 8, 256))
            for b in range(8):
                plan.append((f"wo{b}", w_out_d[:, 256 * b:256 * b + 256].rearrange("(k p) n -> p k n", p=128), 16, 256))
            for g in range(3):
                nch = 16 if g < 2 else 12
                for pr in range(nch // 2):
                    j0 = 16 * g + 2 * pr
                    plan.append((f"upg{j0}", w_up_d[:, 128 * j0:128 * j0 + 256].rearrange("(k p) n -> p k n", p=128), 16, 256))
                    plan.append((f"upv{j0}", w_up_d[:, DFF + 128 * j0:DFF + 128 * j0 + 256].rearrange("(k p) n -> p k n", p=128), 16, 256))
                for cb in range(4):
                    for hf in range(nch // 8 + (1 if nch % 8 else 0)):
                        k0 = 16 * g + 8 * hf
                        nk = min(8, 16 * g + nch - k0)
                        plan.append((f"dn{g}_{cb}_{hf}",
                                     w_down_d[128 * k0:128 * (k0 + nk), 512 * cb:512 * cb + 512].rearrange("(k p) n -> p k n", p=128),
                                     nk, 512))

        for _ in range(ntiles):
            plan_tile()
        wst = dict(issued=0, next=0, released=[False] * len(plan))

        def wview(slot, kd, ncol):
            return wr[:, slot, 0:kd * ncol].rearrange("p (k n) -> p k n", k=kd)

        def wpump():
            while wst["issued"] < len(plan):
                n = wst["issued"]
                if n >= NS and not wst["released"][n - NS]:
                    break
                tag, src, kd, ncol = plan[n]
                slot = n % NS
                dst = wview(slot, kd, ncol)
                S.add("pool", lambda e, dst=dst, src=src: e.dma_start(out=dst, in_=src),
                      writes=[("w", slot)], dma_key=("w", slot))
                wst["issued"] += 1

        def wacq(tag):
            n = wst["next"]
            assert plan[n][0] == tag, (plan[n][0], tag)
            wpump()
            assert wst["issued"] > n, "weight ring deadlock"
            wst["next"] += 1
            _, _, kd, ncol = plan[n]
            return n, n % NS, wview(n % NS, kd, ncol)

        def wrel(n):
            wst["released"][n] = True
            wpump()

        tff = tf[:].rearrange("p a b -> p (a b)")
        S.add("sp", lambda e: e.dma_start(out=pp[:], in_=pp_d), writes=["pp"], dma_key="c0")
        S.add("sp", lambda e: e.dma_start(out=tff[:, 0:640], in_=cmf_d), writes=[("tf", 0), ("tf", 1)], dma_key="c1")
        S.add("sp", lambda e: e.dma_start(out=b2[:].rearrange("p a b -> p (a b)"), in_=bs_d), writes=["b2"], dma_key="c2")
        S.add("sp", lambda e: e.dma_start(out=tff[:, 1024:2048], in_=wT_d), writes=[("tf", 2), ("tf", 3)], dma_key="c3")
        S.add("sp", lambda e: e.dma_start(out=tff[:, 2048:2304], in_=lamb_d), writes=[("tf", 4)], dma_key="c4")
        wpump()
        S.add("dve", lambda e: e.tensor_copy(out=cm[:].rearrange("p a b -> p (a b)"), in_=tff[:, 0:640]),
              reads=[("tf", 0), ("tf", 1)], writes=["cm"])
        S.add("dve", lambda e: e.memset(carry[:], 0.0), writes=["carry"])
        S.add("dve", lambda e: e.memset(col(sc, SC_EPS), EPS), writes=["sc_eps"])
        S.add("dve", lambda e: e.memset(sc[:, SC_M0:SC_M1 + 1], 0.0), writes=["sc_m"])
        S.add("dve", lambda e: e.memset(sc[0:64, SC_M0:SC_M0 + 1], 1.0), reads=["sc_m"], writes=["sc_m"])
        S.add("dve", lambda e: e.memset(sc[64:128, SC_M1:SC_M1 + 1], 1.0), reads=["sc_m"], writes=["sc_m"])
        for g in range(8):
            S.add("dve", lambda e, g=g: e.tensor_tensor(out=wTm[:, g, :], in0=tff[:, 1024 + 128 * g:1024 + 128 * g + 128],
                                                        in1=tff[:, TRI * 128:TRI * 128 + 128], op=ALU.mult),
                  reads=[("tf", 0), ("tf", 1), ("tf", 2), ("tf", 3)], writes=["wTm"])
        for g in range(8):
            bk = nb()
            S.add("pe", lambda e, g=g, bk=bk: e.matmul(ps[:, bk, 0:128], lhsT=cm[:, ONES, :], rhs=wTm[:, g, :], start=True, stop=True),
                  reads=["cm", "wTm"], writes=[PS(bk)])
            S.add("dve", lambda e, g=g, bk=bk: e.scalar_tensor_tensor(out=b2[:, g, :], in0=ps[:, bk, 0:128], scalar=col(pp, PP_LNB + g),
                                                                      in1=b2[:, g, :], op0=ALU.mult, op1=ALU.add),
                  reads=[PS(bk), "pp", "b2"], writes=["b2"])
        S.add("dve", lambda e: e.tensor_scalar(out=hb[:], in0=pp[:, PP_BG:PP_BG + 32], scalar1=0.5, scalar2=None, op0=ALU.mult),
              reads=["pp"], writes=["hb"])
        S.add("dve", lambda e: e.tensor_scalar(out=sc[:, SC_GQA:SC_GQB + 1], in0=pp[:, PP_GQA:PP_GQB + 1], scalar1=0.125, scalar2=None, op0=ALU.mult),
              reads=["pp"], writes=["sc_g"])
        S.add("dve", lambda e: e.tensor_copy(out=sc[:, SC_GKA:SC_GKB + 1], in_=pp[:, PP_GKA:PP_GKB + 1]),
              reads=["pp"], writes=["sc_g"])
        S.add("dve", lambda e: e.tensor_scalar(out=col(sc, SC_SUBG), in0=col(pp, PP_SUBG), scalar1=0.8, scalar2=None, op0=ALU.mult),
              reads=["pp"], writes=["sc_g"])
        S.add("dve", lambda e: e.tensor_tensor(out=tff[:, 2048:2112], in0=tff[:, 2048:2112], in1=tff[:, 2112:2176], op=ALU.mult),
              reads=[("tf", 4)], writes=[("tf", 4)])
        S.add("dve", lambda e: e.tensor_tensor(out=tff[:, 2176:2240], in0=tff[:, 2176:2240], in1=tff[:, 2240:2304], op=ALU.mult),
              reads=[("tf", 4)], writes=[("tf", 4)])
        S.add("dve", lambda e: e.reduce_sum(out=col(sc, SC_T0), in_=tff[:, 2048:2112], axis=mybir.AxisListType.X),
              reads=[("tf", 4)], writes=["sc_t"])
        S.add("dve", lambda e: e.reduce_sum(out=col(sc, SC_T1), in_=tff[:, 2176:2240], axis=mybir.AxisListType.X),
              reads=[("tf", 4)], writes=["sc_t"])
        S.add("act", lambda e: e.activation(out=sc[:, SC_T0:SC_T1 + 1], in_=sc[:, SC_T0:SC_T1 + 1], func=AF.Exp),
              reads=["sc_t"], writes=["sc_t"])
        S.add("dve", lambda e: e.tensor_tensor(out=col(sc, SC_T2), in0=col(sc, SC_T1), in1=col(sc, SC_T0), op=ALU.subtract),
              reads=["sc_t"], writes=["sc_t2"])
        S.add("dve", lambda e: e.tensor_scalar(out=col(sc, SC_NLAM), in0=col(sc, SC_T2), scalar1=-0.2, scalar2=None, op0=ALU.add),
              reads=["sc_t2"], writes=["sc_nlam"])
        SCALL = ["sc_g", "sc_nlam", "sc_eps"]

        def norm_to_T(gcol0, tagrd):
            for c in range(4):
                junk = ar[:, 40:44, :].rearrange("p a b -> p (a b)")
                S.add("act", lambda e, c=c, junk=junk: e.activation(out=junk, in_=hh[:, c, :], func=AF.Square, accum_out=col(stt, c)),
                      reads=HK(c), writes=[SL(40), SL(41), SL(42), SL(43), ("stt", c)])
                S.add("act", lambda e, c=c: e.activation(out=col(stt, 4 + c), in_=col(stt, c), func=AF.Ln, scale=1.0 / D, bias=col(sc, SC_EPS)),
                      reads=[("stt", c), "sc_eps"], writes=[("stt", 4 + c)])
                S.add("act", lambda e, c=c: e.activation(out=col(stt, 8 + c), in_=col(stt, 4 + c), func=AF.Exp, scale=-0.5),
                      reads=[("stt", 4 + c)], writes=[("stt", 8 + c)])
                s0 = 32 + 4 * (c % 2)
                stg = ar[:, s0:s0 + 4, :].rearrange("p a b -> p (a b)")
                S.add("dve", lambda e, c=c, stg=stg: e.tensor_scalar(out=stg, in0=hh[:, c, :], scalar1=col(stt, 8 + c), scalar2=None, op0=ALU.mult),
                      reads=HK(c) + [("stt", 8 + c)], writes=[SL(s0 + q) for q in range(4)])
                for G in range(4):
                    bk = nb()
                    pv = ps[:, bk, :].bitcast(BF16)
                    for q in range(4):
                        kc = 4 * G + q
                        S.add("pe", lambda e, pv=pv, q=q, kc=kc, stg=stg: e.transpose(out=pv[:, q * 128:(q + 1) * 128], in_=stg[:, kc * 128:(kc + 1) * 128],
                                                                                     identity=cm[:, IDENT, :]),
                              reads=[SL(s0 + kc // 4), "cm"], writes=[PS(bk)])
                    gb = pp[:, gcol0 + 4 * G:gcol0 + 4 * G + 4].unsqueeze(2).to_broadcast([128, 4, 128])
                    S.add("dve", lambda e, pv=pv, G=G, c=c, gb=gb: e.tensor_tensor(
                        out=ar[:, 4 * G:4 * G + 4, c * 128:(c + 1) * 128],
                        in0=pv[:, 0:512].rearrange("p (a b) -> p a b", a=4), in1=gb, op=ALU.mult),
                          reads=[PS(bk), "pp"], writes=[SL(4 * G + q) for q in range(4)])

        def dump(name, ap, shape, reads):
            d = nc.dram_tensor("dbg_" + name, list(shape), ap.dtype, kind="ExternalOutput").ap()
            dbg_d[name] = d
            S.add("sp", lambda e: e.dma_start(out=d, in_=ap), reads=reads, dma_key="dbg_" + name)

        ALLSL = [SL(i) for i in range(48)]

        for ti in range(ntiles if stop_after != "setup" else 0):
            tok0 = ti * T
            for c in range(4):
                S.add("sp", lambda e, c=c, tok0=tok0: e.dma_start(out=hh[:, c, :], in_=x_d[tok0 + c * 128:tok0 + (c + 1) * 128, :]),
                      writes=HK(c), dma_key=("x", c))
            S.add("sp", lambda e, tok0=tok0: e.dma_start(out=cs[:, 0, :], in_=cos_d[:, tok0:tok0 + T]), writes=["cos"], dma_key="cos")
            S.add("sp", lambda e, tok0=tok0: e.dma_start(out=cs[:, 1, :], in_=sin_d[:, tok0:tok0 + T]), writes=["sin"], dma_key="sin")
            norm_to_T(PP_N1G, "n1")
            if stop_after == "norm1":
                break

            pend = []

            def qk_post2(bz, zsq, zb, is_q, hd):
                bs_ = nb()
                br_ = nb()
                S.add("pe", lambda e: e.matmul(ps[:, bs_, :], lhsT=cm[:, BONES, :], rhs=tb[:, zsq, :], start=True, stop=True),
                      reads=["cm", ("tb", zsq)], writes=[PS(bs_)])
                S.add("pe", lambda e: e.matmul(ps[:, br_, :], lhsT=cm[:, ROT, :], rhs=tb[:, zb, :], start=True, stop=True),
                      reads=["cm", ("tb", zb)], writes=[PS(br_)])
                rs = ntf()
                t1 = ntf()
                t2 = ntf()
                S.add("act", lambda e: e.activation(out=tf[:, rs, :], in_=ps[:, bs_, :], func=AF.Ln, scale=1.0 / 64, bias=col(sc, SC_EPS)),
                      reads=[PS(bs_), "sc_eps"], writes=[("tf", rs)])
                S.add("act", lambda e: e.activation(out=tf[:, rs, :], in_=tf[:, rs, :], func=AF.Exp, scale=-0.5),
                      reads=[("tf", rs)], writes=[("tf", rs)])
                ga = SC_GQA if is_q else SC_GKA
                S.add("dve", lambda e: e.scalar_tensor_tensor(out=tf[:, t1, :], in0=ps[:, bz, :], scalar=col(sc, ga), in1=cs[:, 0, :],
                                                              op0=ALU.mult, op1=ALU.mult),
                      reads=[PS(bz), "cos"] + SCALL, writes=[("tf", t1)])
                S.add("dve", lambda e: e.scalar_tensor_tensor(out=tf[:, t2, :], in0=ps[:, br_, :], scalar=col(sc, ga + 1), in1=cs[:, 1, :],
                                                              op0=ALU.mult, op1=ALU.mult),
                      reads=[PS(br_), "sin"] + SCALL, writes=[("tf", t2)])
                S.add("dve", lambda e: e.tensor_tensor(out=tf[:, t1, :], in0=tf[:, t1, :], in1=tf[:, t2, :], op=ALU.add),
                      reads=[("tf", t1), ("tf", t2)], writes=[("tf", t1)])
                if is_q:
                    dst = ar[:, 16 + hd, :]
                    wk = [SL(16 + hd)]
                else:
                    dst = kT[:, hd, tok0:tok0 + T]
                    wk = [("kT", hd, ti)]
                S.add("dve", lambda e: e.tensor_tensor(out=dst, in0=tf[:, t1, :], in1=tf[:, rs, :], op=ALU.mult),
                      reads=[("tf", t1), ("tf", rs)], writes=wk)

            for is_q, nm in ((True, "q"), (False, "k")):
                for b in range(4):
                    n, slot, wv = wacq(f"{nm}{b}")
                    for half in range(2):
                        hd = 2 * b + half
                        bz = nb()
                        for kc in range(16):
                            S.add("pe", lambda e, bz=bz, kc=kc, wv=wv, half=half: e.matmul(
                                ps[:, bz, :], lhsT=wv[:, kc, half * 128:(half + 1) * 128], rhs=ar[:, kc, :], start=(kc == 0), stop=(kc == 15)),
                                  reads=[("w", slot), SL(kc)], writes=[PS(bz)])
                        zsq = ntb()
                        zb = ntb()
                        S.add("act", lambda e, bz=bz, zsq=zsq: e.activation(out=tb[:, zsq, :], in_=ps[:, bz, :], func=AF.Square),
                              reads=[PS(bz)], writes=[("tb", zsq)])
                        S.add("act", lambda e, bz=bz, zb=zb: e.activation(out=tb[:, zb, :], in_=ps[:, bz, :], func=AF.Copy),
                              reads=[PS(bz)], writes=[("tb", zb)])
                        for f in pend:
                            f()
                        pend = [lambda bz=bz, zsq=zsq, zb=zb, is_q=is_q, hd=hd: qk_post2(bz, zsq, zb, is_q, hd)]
                    wrel(n)
            for b in range(4):
                n, slot, wv = wacq(f"v{b}")
                for c in range(4):
                    bk = nb()
                    for kc in range(16):
                        S.add("pe", lambda e, bk=bk, kc=kc, wv=wv, c=c: e.matmul(
                            ps[:, bk, 0:256], lhsT=ar[:, kc, c * 128:(c + 1) * 128], rhs=wv[:, kc, :], start=(kc == 0), stop=(kc == 15)),
                              reads=[("w", slot), SL(kc)], writes=[PS(bk)])
                    if b == 0 and c == 0:
                        for f in pend:
                            f()
                        pend = []
                    dst = vv[:, 4 * ti + c, b * 256:(b + 1) * 256]
                    if c % 2 == 0:
                        S.add("act", lambda e, bk=bk, dst=dst: e.activation(out=dst, in_=ps[:, bk, 0:256], func=AF.Copy),
                              reads=[PS(bk)], writes=[("v", 4 * ti + c)])
                    else:
                        S.add("dve", lambda e, bk=bk, dst=dst: e.tensor_copy(out=dst, in_=ps[:, bk, 0:256]),
                              reads=[PS(bk)], writes=[("v", 4 * ti + c)])
                wrel(n)
            for b in range(4):
                n, slot, wv = wacq(f"u{b}")
                for half in range(2):
                    g = 2 * b + half
                    bk = nb()
                    for kc in range(16):
                        S.add("pe", lambda e, bk=bk, kc=kc, wv=wv, half=half: e.matmul(
                            ps[:, bk, :], lhsT=wv[:, kc, half * 128:(half + 1) * 128], rhs=ar[:, kc, :], start=(kc == 0), stop=(kc == 15)),
                              reads=[("w", slot), SL(kc)], writes=[PS(bk)])
                    S.add("act", lambda e, bk=bk, g=g: e.activation(out=ar[:, 24 + g, :], in_=ps[:, bk, :], func=AF.Gelu),
                          reads=[PS(bk)], writes=[SL(24 + g)])
                wrel(n)
            S.add("dve", lambda e: e.memset(stt[:, 16:48], 0.0), writes=["stt_sv"])

            def vn_ap(c, c0, c1):
                return ar[:, 32 + 2 * c:34 + 2 * c, :].rearrange("p a b -> p (a b)")[:, c0:c1]

            for b in range(4):
                n, slot, wv = wacq(f"sv{b}")
                for c in range(4):
                    bk = nb()
                    for kc in range(16):
                        S.add("pe", lambda e, bk=bk, kc=kc, wv=wv, c=c: e.matmul(
                            ps[:, bk, 0:256], lhsT=ar[:, kc, c * 128:(c + 1) * 128], rhs=wv[:, kc, :], start=(kc == 0), stop=(kc == 15)),
                              reads=[("w", slot), SL(kc)], writes=[PS(bk)])
                    dst = vn_ap(c, b * 256, (b + 1) * 256)
                    S.add("act", lambda e, bk=bk, dst=dst, c=c, b=b: e.activation(out=dst, in_=ps[:, bk, 0:256], func=AF.Gelu,
                                                                              accum_out=col(stt, 16 + 4 * c + b)),
                          reads=[PS(bk), "stt_sv"], writes=[SL(32 + 2 * c), SL(33 + 2 * c), ("sts", c, b)])
                    jk = ntb()
                    S.add("act", lambda e, dst=dst, jk=jk, c=c, b=b: e.activation(out=tb[:, jk, 0:256], in_=dst, func=AF.Square,
                                                                              accum_out=col(stt, 32 + 4 * c + b)),
                          reads=[SL(32 + 2 * c), SL(33 + 2 * c), "stt_sv"], writes=[("tb", jk), ("stq", c, b)])
                wrel(n)
            def sv_stats_a():
                rd = [("sts", c, b) for c in range(4) for b in range(4)] + [("stq", c, b) for c in range(4) for b in range(4)]
                X = mybir.AxisListType.X
                S.add("dve", lambda e: e.reduce_sum(out=stt[:, 48:52], in_=stt[:, 16:32].rearrange("p (c b) -> p c b", b=4), axis=X),
                      reads=rd, writes=["svA"])
                S.add("dve", lambda e: e.reduce_sum(out=stt[:, 52:56], in_=stt[:, 32:48].rearrange("p (c b) -> p c b", b=4), axis=X),
                      reads=rd + ["svA"], writes=["svA"])
                S.add("dve", lambda e: e.tensor_scalar(out=stt[:, 48:52], in0=stt[:, 48:52], scalar1=1.0 / 1024, scalar2=None, op0=ALU.mult),
                      reads=["svA"], writes=["svA"])
                S.add("dve", lambda e: e.tensor_tensor(out=stt[:, 56:60], in0=stt[:, 48:52], in1=stt[:, 48:52], op=ALU.mult),
                      reads=["svA"], writes=["svA"])
                S.add("dve", lambda e: e.scalar_tensor_tensor(out=stt[:, 56:60], in0=stt[:, 52:56], scalar=1.0 / 1024, in1=stt[:, 56:60],
                                                              op0=ALU.mult, op1=ALU.subtract),
                      reads=["svA"], writes=["svA"])

            def sv_stats_b():
                S.add("act", lambda e: e.activation(out=stt[:, 60:64], in_=stt[:, 56:60], func=AF.Ln, bias=col(sc, SC_EPS)),
                      reads=["svA", "sc_eps"], writes=["svB"])
                S.add("act", lambda e: e.activation(out=stt[:, 60:64], in_=stt[:, 60:64], func=AF.Exp, scale=-0.5),
                      reads=["svB"], writes=["svB"])
                for c in range(4):
                    vfull = vn_ap(c, 0, 1024)
                    S.add("dve", lambda e, c=c, vfull=vfull: e.tensor_scalar(out=vfull, in0=vfull, scalar1=col(stt, 48 + c), scalar2=col(stt, 60 + c),
                                                                             op0=ALU.subtract, op1=ALU.mult),
                          reads=[SL(32 + 2 * c), SL(33 + 2 * c), "svA", "svB"], writes=[SL(32 + 2 * c), SL(33 + 2 * c)])

            sv_stats_a()
            if stop_after == "proj":
                sv_stats_b()
                break

            nj = 4 * (ti + 1)
            late = []
            def qz_prep(hq):
                z0 = 40 + 2 * (hq % 4)
                S.add("dve", lambda e: e.tensor_scalar(out=ar[:, z0, :], in0=ar[:, 16 + hq, :], scalar1=col(sc, SC_M0), scalar2=None, op0=ALU.mult),
                      reads=[SL(16 + hq), "sc_m"], writes=[SL(z0)])
                S.add("dve", lambda e: e.tensor_scalar(out=ar[:, z0 + 1, :], in0=ar[:, 16 + hq, :], scalar1=col(sc, SC_M1), scalar2=None, op0=ALU.mult),
                      reads=[SL(16 + hq), "sc_m"], writes=[SL(z0 + 1)])

            qz_prep(0)
            for hd in range(8):
                pav = []
                qz0 = 40 + 2 * (hd % 4)
                qz1 = qz0 + 1
                for j in range(nj):
                    r = j - 4 * ti
                    c0 = 128 * r if r > 0 else 0
                    pr = cnt["pair"] % 2
                    cnt["pair"] += 1
                    sb0, sb1 = 2 * pr, 2 * pr + 1
                    ksl = slice(j * 128, (j + 1) * 128)
                    kkey = ("kT", hd, j // 4)
                    S.add("pe", lambda e, sb0=sb0, c0=c0, ksl=ksl, hd=hd, qz0=qz0: e.matmul(
                        ps[:, sb0, c0:512], lhsT=kT[:, hd, ksl], rhs=ar[:, qz0, c0:512], start=True, stop=True),
                          reads=[kkey, SL(qz0)], writes=[PS(sb0)])
                    S.add("pe", lambda e, sb1=sb1, c0=c0, ksl=ksl, hd=hd, qz1=qz1: e.matmul(
                        ps[:, sb1, c0:512], lhsT=kT[:, hd, ksl], rhs=ar[:, qz1, c0:512], start=True, stop=True),
                          reads=[kkey, SL(qz1)], writes=[PS(sb1)])
                    if j == 0 and hd < 7:
                        qz_prep(hd + 1)
                    p0 = ntb()
                    p1 = ntb()
                    S.add("act", lambda e, sb0=sb0, p0=p0, c0=c0: e.activation(out=tb[:, p0, c0:512], in_=ps[:, sb0, c0:512], func=AF.Exp),
                          reads=[PS(sb0)], writes=[("tb", p0)])
                    S.add("act", lambda e, sb1=sb1, p1=p1, c0=c0: e.activation(out=tb[:, p1, c0:512], in_=ps[:, sb1, c0:512], func=AF.Exp),
                          reads=[PS(sb1)], writes=[("tb", p1)])
                    if r >= 0:
                        for pq in (p0, p1):
                            S.add("dve", lambda e, pq=pq, c0=c0: e.tensor_tensor(out=tb[:, pq, c0:c0 + 128], in0=tb[:, pq, c0:c0 + 128],
                                                                                 in1=cm[:, TRI, :], op=ALU.mult),
                                  reads=[("tb", pq), "cm"], writes=[("tb", pq)])
                    for f in pav:
                        f()

                    def av(j=j, c0=c0, p0=p0, p1=p1, hd=hd, nj=nj):
                        vsl = vv[:, j, hd * 128:(hd + 1) * 128]
                        for (bo, bsum, pq) in ((4, 6, p0), (5, 7, p1)):
                            S.add("pe", lambda e, bo=bo, pq=pq, vsl=vsl: e.matmul(
                                ps[:, bo, c0:512], lhsT=vsl, rhs=tb[:, pq, c0:512], start=(j == 0), stop=(j == nj - 1)),
                                  reads=[("v", j), ("tb", pq)], writes=[PS(bo)])
                            S.add("pe", lambda e, bsum=bsum, pq=pq: e.matmul(
                                ps[:, bsum, c0:512], lhsT=cm[:, ONES, :], rhs=tb[:, pq, c0:512], start=(j == 0), stop=(j == nj - 1)),
                                  reads=["cm", ("tb", pq)], writes=[PS(bsum)])
                    pav = [av]
                    if j == min(6, nj - 1):
                        for f in late:
                            f()
                        late = []
                for f in pav:
                    f()
                r0 = ntf()
                rb = ntf()
                rc = ntf()
                r1 = ntf()
                S.add("act", lambda e, r0=r0: e.activation(out=tf[:, r0, :], in_=ps[:, 4, :], func=AF.Copy), reads=[PS(4)], writes=[("tf", r0)])
                S.add("dve", lambda e, rb=rb: e.tensor_copy(out=tf[:, rb, :], in_=ps[:, 5, :]), reads=[PS(5)], writes=[("tf", rb)])
                S.add("act", lambda e, rc=rc: e.activation(out=tf[:, rc, :], in_=ps[:, 6, :], func=AF.Ln), reads=[PS(6)], writes=[("tf", rc)])
                S.add("dve", lambda e, r1=r1: e.tensor_copy(out=tf[:, r1, :], in_=ps[:, 7, :]), reads=[PS(7)], writes=[("tf", r1)])
                S.add("act", lambda e, rc=rc: e.activation(out=tf[:, rc, :], in_=tf[:, rc, :], func=AF.Exp, scale=-1.0),
                      reads=[("tf", rc)], writes=[("tf", rc)])
                S.add("act", lambda e, r1=r1: e.activation(out=tf[:, r1, :], in_=tf[:, r1, :], func=AF.Ln), reads=[("tf", r1)], writes=[("tf", r1)])
                S.add("act", lambda e, r1=r1: e.activation(out=tf[:, r1, :], in_=tf[:, r1, :], func=AF.Exp, scale=-1.0),
                      reads=[("tf", r1)], writes=[("tf", r1)])
                S.add("dve", lambda e, r0=r0, rc=rc: e.tensor_tensor(out=tf[:, r0, :], in0=tf[:, r0, :], in1=tf[:, rc, :], op=ALU.mult),
                      reads=[("tf", r0), ("tf", rc)], writes=[("tf", r0)])
                S.add("dve", lambda e, rb=rb, r1=r1: e.tensor_tensor(out=tf[:, rb, :], in0=tf[:, rb, :], in1=tf[:, r1, :], op=ALU.mult),
                      reads=[("tf", rb), ("tf", r1)], writes=[("tf", rb)])
                S.add("dve", lambda e, r0=r0, rb=rb: e.scalar_tensor_tensor(out=tf[:, r0, :], in0=tf[:, rb, :], scalar=col(sc, SC_NLAM), in1=tf[:, r0, :],
                                                                            op0=ALU.mult, op1=ALU.add),
                      reads=[("tf", r0), ("tf", rb)] + SCALL, writes=[("tf", r0)])
                if debug and hd == 0 and ti == 0:
                    dump("o0", tf[:, r0, :], [128, 512], [("tf", r0)])
                S.add("act", lambda e, r0=r0: e.activation(out=osqb[:], in_=tf[:, r0, :], func=AF.Square),
                      reads=[("tf", r0)], writes=["osq"])

                def subln(r0=r0, r1=r1, hd=hd):
                    pr = cnt["pair"] % 2
                    cnt["pair"] += 1
                    bk = 2 * pr
                    S.add("pe", lambda e: e.matmul(ps[:, bk, :], lhsT=cm[:, ONES, :], rhs=osqb[:], start=True, stop=True),
                          reads=["cm", "osq"], writes=[PS(bk)])
                    S.add("act", lambda e: e.activation(out=tf[:, r1, :], in_=ps[:, bk, :], func=AF.Ln, scale=1.0 / 128, bias=col(sc, SC_EPS)),
                          reads=[PS(bk), "sc_eps"], writes=[("tf", r1)])
                    S.add("act", lambda e: e.activation(out=tf[:, r1, :], in_=tf[:, r1, :], func=AF.Exp, scale=-0.5),
                          reads=[("tf", r1)], writes=[("tf", r1)])
                    S.add("dve", lambda e: e.scalar_tensor_tensor(out=ar[:, 16 + hd, :], in0=tf[:, r0, :], scalar=col(sc, SC_SUBG), in1=tf[:, r1, :],
                                                                  op0=ALU.mult, op1=ALU.mult),
                          reads=[("tf", r0), ("tf", r1)] + SCALL, writes=[SL(16 + hd)])
                late = [subln]
                if hd == 0:
                    sv_stats_b()
            for g in range(8):
                bk = nb()
                for c in range(4):
                    S.add("pe", lambda e, bk=bk, c=c, g=g: e.matmul(
                        ps[:, bk, c * 128:(c + 1) * 128], lhsT=vn_ap(c, g * 128, (g + 1) * 128), rhs=wTm[:, g, :], start=True, stop=True),
                          reads=[SL(32 + 2 * c), SL(33 + 2 * c), "wTm"], writes=[PS(bk)])
                b2b = b2[:, g, :].unsqueeze(1).broadcast_to([128, 4, 128])
                psv = ps[:, bk, :].rearrange("p (a b) -> p a b", a=4)
                S.add("dve", lambda e, psv=psv, g=g, b2b=b2b: e.scalar_tensor_tensor(
                    out=psv, in0=psv, scalar=col(pp, PP_LNG + g), in1=b2b, op0=ALU.mult, op1=ALU.add),
                      reads=[PS(bk), "pp", "b2"], writes=[PS(bk)])
                S.add("dve", lambda e, g=g, bk=bk: e.tensor_tensor(out=ar[:, 24 + g, :], in0=ps[:, bk, :], in1=ar[:, 24 + g, :], op=ALU.mult),
                      reads=[PS(bk), SL(24 + g)], writes=[SL(24 + g)])
            if stop_after == "mix":
                for f in late:
                    f()
                late = []
                break

            for jp in range(8):
                res = {}
                for nm, kd, src0 in (("ga", 16, 0), ("gs", 16, 0), ("att", 8, 16), ("sgu", 8, 24)):
                    n, slot, wv = wacq(f"{nm}{jp}")
                    for half in range(2):
                        bk = nb()
                        for kc in range(kd):
                            S.add("pe", lambda e, bk=bk, kc=kc, wv=wv, half=half, kd=kd, src0=src0: e.matmul(
                                ps[:, bk, :], lhsT=wv[:, kc, half * 128:(half + 1) * 128], rhs=ar[:, src0 + kc, :], start=(kc == 0), stop=(kc == kd - 1)),
                                  reads=[("w", slot), SL(src0 + kc)], writes=[PS(bk)])
                        res[(nm, half)] = bk
                        if late and half == 1:
                            for f in late:
                                f()
                            late = []
                        if nm in ("ga", "gs"):
                            j = 2 * jp + half
                            bcol = j if nm == "ga" else 16 + j
                            t = ntf()
                            S.add("act", lambda e, bk=bk, t=t, bcol=bcol: e.activation(out=tf[:, t, :], in_=ps[:, bk, :], func=AF.Tanh, scale=0.5,
                                                                                   bias=col(hb, bcol)),
                                  reads=[PS(bk), "hb"], writes=[("tf", t)])
                            res[(nm + "t", half)] = t
                    wrel(n)
                for half in range(2):
                    j = 2 * jp + half
                    ta, ts_ = res[("gat", half)], res[("gst", half)]
                    ba, bs_ = res[("att", half)], res[("sgu", half)]
                    S.add("dve", lambda e, ta=ta, ba=ba: e.scalar_tensor_tensor(out=tf[:, ta, :], in0=tf[:, ta, :], scalar=1.0, in1=ps[:, ba, :],
                                                                                op0=ALU.add, op1=ALU.mult),
                          reads=[("tf", ta), PS(ba)], writes=[("tf", ta)])
                    S.add("dve", lambda e, ts_=ts_, bs_=bs_: e.scalar_tensor_tensor(out=tf[:, ts_, :], in0=tf[:, ts_, :], scalar=1.0, in1=ps[:, bs_, :],
                                                                                    op0=ALU.add, op1=ALU.mult),
                          reads=[("tf", ts_), PS(bs_)], writes=[("tf", ts_)])
                    S.add("dve", lambda e, ta=ta, ts_=ts_, j=j: e.tensor_tensor(out=ar[:, 32 + j, :], in0=tf[:, ta, :], in1=tf[:, ts_, :], op=ALU.add),
                          reads=[("tf", ta), ("tf", ts_)], writes=[SL(32 + j)])
            for b in range(8):
                n, slot, wv = wacq(f"wo{b}")
                for c in range(4):
                    bk = nb()
                    for kc in range(16):
                        S.add("pe", lambda e, bk=bk, kc=kc, wv=wv, c=c: e.matmul(
                            ps[:, bk, 0:256], lhsT=ar[:, 32 + kc, c * 128:(c + 1) * 128], rhs=wv[:, kc, :], start=(kc == 0), stop=(kc == 15)),
                              reads=[("w", slot), SL(32 + kc)], writes=[PS(bk)])
                    hs = hh[:, c, b * 256:(b + 1) * 256]
                    S.add("dve", lambda e, bk=bk, hs=hs: e.scalar_tensor_tensor(out=hs, in0=ps[:, bk, 0:256], scalar=0.5, in1=hs, op0=ALU.mult, op1=ALU.add),
                          reads=[PS(bk), ("h", c, b // 2)], writes=[("h", c, b // 2)])
                wrel(n)
            if stop_after == "mixer":
                break
            norm_to_T(PP_N2G, "n2")
            for g in range(3):
                nch = 16 if g < 2 else 12
                for pr in range(nch // 2):
                    j0 = 16 * g + 2 * pr
                    ng, sg, wg = wacq(f"upg{j0}")
                    nv, sv_, wvv = wacq(f"upv{j0}")
                    for half in range(2):
                        jj = j0 + half
                        tt = {}
                        for kind, slot, wv, chn in (("g", sg, wg, jj), ("v", sv_, wvv, NHC + jj)):
                            bk = nb()
                            for kc in range(16):
                                S.add("pe", lambda e, bk=bk, kc=kc, wv=wv, half=half: e.matmul(
                                    ps[:, bk, :], lhsT=wv[:, kc, half * 128:(half + 1) * 128], rhs=ar[:, kc, :], start=(kc == 0), stop=(kc == 15)),
                                      reads=[("w", slot), SL(kc)], writes=[PS(bk)])
                            t0 = ntf()
                            w0 = col(pp, PP_CW + 3 * chn + 0)
                            w1 = col(pp, PP_CW + 3 * chn + 1)
                            w2 = col(pp, PP_CW + 3 * chn + 2)
                            cb_ = col(pp, PP_CB + chn)
                            ck = ("carry", chn)
                            S.add("act", lambda e, bk=bk, t0=t0, w2=w2, cb_=cb_: e.activation(out=tf[:, t0, :], in_=ps[:, bk, :], func=AF.Identity,
                                                                                          scale=w2, bias=cb_),
                                  reads=[PS(bk), "pp"], writes=[("tf", t0)])
                            S.add("dve", lambda e, bk=bk, t0=t0, w1=w1: e.scalar_tensor_tensor(
                                out=tf[:, t0, 1:512], in0=ps[:, bk, 0:511], scalar=w1, in1=tf[:, t0, 1:512], op0=ALU.mult, op1=ALU.add),
                                  reads=[PS(bk), ("tf", t0), "pp"], writes=[("tf", t0)])
                            S.add("dve", lambda e, bk=bk, t0=t0, w0=w0: e.scalar_tensor_tensor(
                                out=tf[:, t0, 2:512], in0=ps[:, bk, 0:510], scalar=w0, in1=tf[:, t0, 2:512], op0=ALU.mult, op1=ALU.add),
                                  reads=[PS(bk), ("tf", t0), "pp"], writes=[("tf", t0)])
                            S.add("dve", lambda e, t0=t0, w0=w0, chn=chn: e.scalar_tensor_tensor(
                                out=tf[:, t0, 0:2], in0=carry[:, chn, :], scalar=w0, in1=tf[:, t0, 0:2], op0=ALU.mult, op1=ALU.add),
                                  reads=[ck, ("tf", t0), "pp", "carry"], writes=[("tf", t0)])
                            S.add("dve", lambda e, t0=t0, w1=w1, chn=chn: e.scalar_tensor_tensor(
                                out=tf[:, t0, 0:1], in0=carry[:, chn, 1:2], scalar=w1, in1=tf[:, t0, 0:1], op0=ALU.mult, op1=ALU.add),
                                  reads=[ck, ("tf", t0), "pp", "carry"], writes=[("tf", t0)])
                            S.add("dve", lambda e, bk=bk, chn=chn: e.tensor_copy(out=carry[:, chn, :], in_=ps[:, bk, 510:512]),
                                  reads=[PS(bk), "carry"], writes=[ck])
                            tt[kind] = t0
                        gl = ntb()
                        S.add("act", lambda e, gl=gl, tg=tt["g"]: e.activation(out=tb[:, gl, :], in_=tf[:, tg, :], func=AF.Gelu),
                              reads=[("tf", tt["g"])], writes=[("tb", gl)])
                        hsl = 16 + (jj - 16 * g)
                        S.add("dve", lambda e, gl=gl, tv=tt["v"], hsl=hsl: e.tensor_tensor(out=ar[:, hsl, :], in0=tb[:, gl, :], in1=tf[:, tv, :], op=ALU.mult),
                              reads=[("tb", gl), ("tf", tt["v"])], writes=[SL(hsl)])
                    wrel(ng)
                    wrel(nv)
                for cb in range(4):
                    nhf = nch // 8 + (1 if nch % 8 else 0)
                    banks = [nb() for _ in range(4)]
                    for hf in range(nhf):
                        n, slot, wv = wacq(f"dn{g}_{cb}_{hf}")
                        nk = min(8, nch - 8 * hf)
                        for c in range(4):
                            for k in range(nk):
                                kl = 8 * hf + k
                                S.add("pe", lambda e, bk=banks[c], k=k, kl=kl, wv=wv, c=c, nch=nch: e.matmul(
                                    ps[:, bk, :], lhsT=ar[:, 16 + kl, c * 128:(c + 1) * 128], rhs=wv[:, k, :], start=(kl == 0), stop=(kl == nch - 1)),
                                      reads=[("w", slot), SL(16 + kl)], writes=[PS(banks[c])])
                        wrel(n)
                    for c in range(4):
                        hs = hh[:, c, cb * 512:(cb + 1) * 512]
                        if g == 2 and cb == 3:
                            stg_ = ntf()
                            S.add("dve", lambda e, bk=banks[c], hs=hs, stg_=stg_: e.tensor_tensor(out=tf[:, stg_, :], in0=ps[:, bk, :], in1=hs, op=ALU.add),
                                  reads=[PS(banks[c]), ("h", c, cb)], writes=[("tf", stg_)])
                            od = out_d[tok0 + c * 128:tok0 + (c + 1) * 128, cb * 512:(cb + 1) * 512]
                            S.add("sp", lambda e, od=od, stg_=stg_: e.dma_start(out=od, in_=tf[:, stg_, :]),
                                  reads=[("tf", stg_)], dma_key=("o", c, cb))
                            continue
                        S.add("dve", lambda e, bk=banks[c], hs=hs: e.tensor_tensor(out=hs, in0=ps[:, bk, :], in1=hs, op=ALU.add),
                              reads=[PS(banks[c]), ("h", c, cb)], writes=[("h", c, cb)])
                        if g == 2:
                            od = out_d[tok0 + c * 128:tok0 + (c + 1) * 128, cb * 512:(cb + 1) * 512]
                            S.add("sp", lambda e, od=od, hs=hs: e.dma_start(out=od, in_=hs),
                                  reads=[("h", c, cb)], dma_key=("o", c, cb))

        if debug:
            dump("kT", kT[:], [128, 8, SEQ], [("kT", h_, t_) for h_ in range(8) for t_ in range(ntiles)])
            dump("vv", vv[:], [128, 16, 1024], [("v", j) for j in range(16)])
            dump("ar", ar[:], [128, 48, 512], ALLSL)
            dump("hh", hh[:], [128, 4, D], [k for c in range(4) for k in HK(c)])
        S.emit(nc)
    return nc, S


def _consts():
    p = np.arange(128)
    ident = np.eye(128, dtype=np.float32)
    rot = np.zeros((128, 128), np.float32)
    for m in range(128):
        d = m % 64
        if d < 32:
            rot[m + 32, m] = -1.0
        else:
            rot[m - 32, m] = 1.0
    bones = (p[:, None] // 64 == p[None, :] // 64).astype(np.float32)
    tri = (p[None, :] >= p[:, None]).astype(np.float32)
    ones = np.ones((128, 128), np.float32)
    cmf = np.stack([ident, rot, bones, tri, ones], axis=1).reshape(128, 5 * 128)
    inv = np.exp(-math.log(10000.0) * np.arange(0, 64, 2, dtype=np.float32) / 64).astype(np.float32)
    ang = np.arange(SEQ, dtype=np.float32)[:, None] * inv[None, :]
    idx = (p % 64) % 32
    cosT = np.cos(ang).astype(np.float32).T[idx]
    sinT = np.sin(ang).astype(np.float32).T[idx]
    return np.ascontiguousarray(cmf), np.ascontiguousarray(cosT), np.ascontiguousarray(sinT)


def _layout_params(inp):
    f = lambda a: np.asarray(a, dtype=np.float32)
    pp = np.zeros((128, PP_END), np.float32)
    pp[:, PP_N1G:PP_N1G + 16] = f(inp["norm1_g"])[0].reshape(16, 128).T
    pp[:, PP_N2G:PP_N2G + 16] = f(inp["norm2_g"])[0].reshape(16, 128).T
    pp[:, PP_BG:PP_BG + 32] = f(inp["b_gate"])[0].reshape(32, 128).T
    d = np.arange(128) % 64
    qg = f(inp["q_norm_g"])[0]
    kg = f(inp["k_norm_g"])[0]
    pp[:, PP_GQA] = qg[d]
    pp[:, PP_GQB] = qg[(d + 32) % 64]
    pp[:, PP_GKA] = kg[d]
    pp[:, PP_GKB] = kg[(d + 32) % 64]
    pp[:, PP_SUBG] = f(inp["subln_g"])[0]
    pp[:, PP_LNG:PP_LNG + 8] = f(inp["sgu_norm_g"])[0].reshape(8, 128).T
    pp[:, PP_LNB:PP_LNB + 8] = f(inp["sgu_norm_b"])[0].reshape(8, 128).T
    cw = f(inp["conv_w"])[0]
    pp[:, PP_CW:PP_CW + 264] = cw.reshape(3, 88, 128).transpose(2, 1, 0).reshape(128, 264)
    pp[:, PP_CB:PP_CB + 88] = f(inp["conv_b"])[0].reshape(88, 128).T
    lamb = np.concatenate([f(inp["lambda_q1"])[0], f(inp["lambda_k1"])[0], f(inp["lambda_q2"])[0], f(inp["lambda_k2"])[0]])
    lamb = np.ascontiguousarray(np.broadcast_to(lamb[None, :], (128, 256)))
    wT = np.ascontiguousarray(f(inp["sgu_w"])[0].transpose(2, 0, 1).reshape(128, 1024))
    bs = np.ascontiguousarray(np.broadcast_to(f(inp["sgu_b"])[0].reshape(1, 1024), (128, 1024)))
    return pp, lamb, wT, bs


_CACHE = {}


def kernel(**inputs):
    f = lambda a: np.ascontiguousarray(np.asarray(a, dtype=np.float32))
    if "nc" not in _CACHE:
        _CACHE["nc"] = build()[0]
    nc = _CACHE["nc"]
    cmf, cosT, sinT = _consts()
    pp, lamb, wT, bs = _layout_params(inputs)
    x = f(inputs["x"])
    shared = {
        "w_in": f(inputs["w_in"])[0], "w_att_out": f(inputs["w_att_out"])[0], "w_sgu_out": f(inputs["w_sgu_out"])[0],
        "w_out": f(inputs["w_out"])[0], "w_up": f(inputs["w_up"])[0], "w_down": f(inputs["w_down"])[0],
        "pp": pp, "lamb": lamb, "cmf": cmf, "sgu_wT": wT, "sgu_bs": bs, "cosT": cosT, "sinT": sinT,
    }
    in_maps = [dict(shared, x=x[b]) for b in range(8)]
    res = run_bass_kernel_spmd(nc, in_maps, core_ids=list(range(8)))
    return np.stack([np.asarray(r["out"], dtype=np.float32) for r in res.results], axis=0)
```

```python
import contextlib
import math
import numpy as np
import concourse.bass as bass
import concourse.mybir as mybir
from concourse.bass_utils import run_bass_kernel_spmd

F32 = mybir.dt.float32
BF16 = mybir.dt.bfloat16
AF = mybir.ActivationFunctionType
ALU = mybir.AluOpType

D = 2048
SEQ = 2048
T = 512
NTILE = 4
INC = 9216
DFF = 5632
NHC = 44
EPS = 1e-6
NS = 4
NTF = 5
NTB = 6
COMPUTE = ("pe", "act", "dve", "pool")

PP_N1G, PP_N2G, PP_BG, PP_GQA, PP_GQB, PP_GKA, PP_GKB, PP_SUBG, PP_LNG, PP_LNB, PP_CW, PP_CB, PP_END = (
    0, 16, 32, 64, 65, 66, 67, 68, 69, 77, 85, 349, 437)


class Sched:
    def __init__(self):
        self.ops = []
        self.last_writer = {}
        self.readers = {}

    def add(self, eng, fn, reads=(), writes=(), dma_key=None):
        idx = len(self.ops)
        deps = set()
        psr = [b for b in reads if isinstance(b, tuple) and b[0] == "ps"]
        if psr:
            reads = [b for b in reads if b not in psr]
            writes = list(writes) + [b for b in psr if b not in writes]
        for b in reads:
            w = self.last_writer.get(b)
            if w is not None:
                deps.add(w)
        for b in writes:
            w = self.last_writer.get(b)
            if w is not None:
                deps.add(w)
            for r in self.readers.get(b, ()):
                deps.add(r)
        for b in reads:
            self.readers.setdefault(b, []).append(idx)
        for b in writes:
            self.last_writer[b] = idx
            self.readers[b] = []
        self.ops.append(dict(eng=eng, fn=fn, deps=deps, dma_key=dma_key, idx=idx))
        return idx

    def emit(self, nc):
        ops = self.ops
        needed = set()
        for o in ops:
            nd = set()
            best = {}
            for d in o["deps"]:
                p = ops[d]
                if p["dma_key"] is not None:
                    nd.add(d)
                    continue
                if p["eng"] == "pe" and o["eng"] == "pe" and o["dma_key"] is None:
                    continue
                if best.get(p["eng"], -1) < d:
                    best[p["eng"]] = d
            nd |= set(best.values())
            o["deps"] = nd
            needed |= nd
        eng_count = {e: 0 for e in COMPUTE}
        key_count = {}
        for o in ops:
            if o["dma_key"] is not None:
                k = o["dma_key"]
                key_count[k] = key_count.get(k, 0) + 16
                o["sem"] = ("dma", k)
                o["cnt"] = key_count[k]
            elif o["idx"] in needed:
                e = o["eng"]
                eng_count[e] += 1
                o["sem"] = ("eng", e)
                o["cnt"] = eng_count[e]
            else:
                o["sem"] = None
        sem_names = [("eng", e) for e in COMPUTE] + [("dma", k) for k in key_count]
        with contextlib.ExitStack() as st:
            sems = {}
            for n_, sn in enumerate(sem_names):
                sems[sn] = st.enter_context(nc.semaphore(f"sem{n_}"))
            block = st.enter_context(nc.Block())
            per_eng = {e: [o for o in ops if o["eng"] == e] for e in COMPUTE + ("sp",)}

            def run(engobj, ename):
                waited = {}
                for o in per_eng[ename]:
                    need = {}
                    for d in o["deps"]:
                        p = ops[d]
                        s = p["sem"]
                        if need.get(s, 0) < p["cnt"]:
                            need[s] = p["cnt"]
                    for s, c in need.items():
                        if waited.get(s, 0) >= c:
                            continue
                        engobj.wait_ge(sems[s], c)
                        waited[s] = c
                    ins = o["fn"](engobj)
                    if o["sem"] is not None:
                        ins.then_inc(sems[o["sem"]], 16 if o["sem"][0] == "dma" else 1)
                if ename == "sp":
                    for k, c in key_count.items():
                        engobj.wait_ge(sems[("dma", k)], c)

            @block.tensor
            def _(e):
                run(e, "pe")

            @block.scalar
            def _(e):
                run(e, "act")

            @block.vector
            def _(e):
                run(e, "dve")

            @block.gpsimd
            def _(e):
                run(e, "pool")

            @block.sync
            def _(e):
                run(e, "sp")


def build(ntiles=NTILE, stop_after=None, debug=False):
    nc = bass.Bass("TRN2", target_bir_lowering=False)

    def din(name, shape):
        return nc.dram_tensor(name, list(shape), F32, kind="ExternalInput").ap()

    x_d = din("x", [SEQ, D])
    w_in_d = din("w_in", [D, INC])
    w_att_d = din("w_att_out", [1024, D])
    w_sgu_d = din("w_sgu_out", [1024, D])
    w_out_d = din("w_out", [D, D])
    w_up_d = din("w_up", [D, 2 * DFF])
    w_down_d = din("w_down", [DFF, D])
    pp_d = din("pp", [128, PP_END])
    lamb_d = din("lamb", [128, 256])
    cmf_d = din("cmf", [128, 5 * 128])
    wT_d = din("sgu_wT", [128, 8 * 128])
    bs_d = din("sgu_bs", [128, 8 * 128])
    cos_d = din("cosT", [128, SEQ])
    sin_d = din("sinT", [128, SEQ])
    out_d = nc.dram_tensor("out", [SEQ, D], F32, kind="ExternalOutput").ap()
    dbg_d = {}

    S = Sched()
    st = contextlib.ExitStack()
    with st:
        def sb(name, shape, dt):
            return st.enter_context(nc.sbuf_tensor("sb_" + name, list(shape), dt))

        kT = sb("kT", [128, 8, SEQ], BF16)
        vv = sb("vv", [128, 16, 1024], BF16)
        ar = sb("ar", [128, 48, 512], BF16)
        hh = sb("hh", [128, 4, D], F32)
        wr = sb("wr", [128, NS, 4096], BF16)
        cm = sb("cm", [128, 5, 128], BF16)
        pp = sb("pp", [128, PP_END], F32)
        cs = sb("cs", [128, 2, 512], F32)
        wTm = sb("wTm", [128, 8, 128], BF16)
        b2 = sb("b2", [128, 8, 128], F32)
        hb = sb("hb", [128, 32], F32)
        sc = sb("sc", [128, 16], F32)
        stt = sb("stt", [128, 64], F32)
        carry = sb("carry", [128, 88, 2], F32)
        tf = sb("tf", [128, NTF, 512], F32)
        tb = sb("tb", [128, NTB, 512], BF16)
        osqb = sb("osqb", [128, 512], BF16)
        ps = st.enter_context(nc.psum_tensor("ps", [128, 8, 512], F32))

        IDENT, ROT, BONES, TRI, ONES = range(5)
        SC_GQA, SC_GQB, SC_GKA, SC_GKB, SC_SUBG, SC_NLAM, SC_EPS, SC_T0, SC_T1, SC_T2, SC_T3, SC_M0, SC_M1 = range(13)

        def col(t, k):
            return t[:, k:k + 1]

        cnt = dict(bank=0, tf=0, tb=0, pair=0)

        def nb():
            b = cnt["bank"] % 8
            cnt["bank"] += 1
            return b

        def ntf():
            i = cnt["tf"] % NTF
            cnt["tf"] += 1
            return i

        def ntb():
            i = cnt["tb"] % NTB
            cnt["tb"] += 1
            return i

        def PS(b):
            return ("ps", b)

        def SL(i):
            return ("sl", i)

        def HK(c):
            return [("h", c, q) for q in range(4)]

        plan = []

        def plan_tile():
            for nm, base in (("q", 0), ("k", 1024), ("v", 2048), ("u", 3072), ("sv", 4096)):
                for b in range(4):
                    c0 = base + 256 * b
                    plan.append((f"{nm}{b}", w_in_d[:, c0:c0 + 256].rearrange("(k p) n -> p k n", p=128), 16, 256))
            for jp in range(8):
                c0 = 5120 + 256 * jp
                plan.append((f"ga{jp}", w_in_d[:, c0:c0 + 256].rearrange("(k p) n -> p k n", p=128), 16, 256))
                c0 = 7168 + 256 * jp
                plan.append((f"gs{jp}", w_in_d[:, c0:c0 + 256].rearrange("(k p) n -> p k n", p=128), 16, 256))
                plan.append((f"att{jp}", w_att_d[:, 256 * jp:256 * jp + 256].rearrange("(k p) n -> p k n", p=128), 8, 256))
                plan.append((f"sgu{jp}", w_sgu_d[:, 256 * jp:256 * jp + 256].rearrange("(k p) n -> p k n", p=128), 8, 256))
            for b in range(8):
                plan.append((f"wo{b}", w_out_d[:, 256 * b:256 * b + 256].rearrange("(k p) n -> p k n", p=128), 16, 256))
            for g in range(3):
                nch = 16 if g < 2 else 12
                for pr in range(nch // 2):
                    j0 = 16 * g + 2 * pr
                    plan.append((f"upg{j0}", w_up_d[:, 128 * j0:128 * j0 + 256].rearrange("(k p) n -> p k n", p=128), 16, 256))
                    plan.append((f"upv{j0}", w_up_d[:, DFF + 128 * j0:DFF + 128 * j0 + 256].rearrange("(k p) n -> p k n", p=128), 16, 256))
                for cb in range(4):
                    for hf in range(nch // 8 + (1 if nch % 8 else 0)):
                        k0 = 16 * g + 8 * hf
                        nk = min(8, 16 * g + nch - k0)
                        plan.append((f"dn{g}_{cb}_{hf}",
                                     w_down_d[128 * k0:128 * (k0 + nk), 512 * cb:512 * cb + 512].rearrange("(k p) n -> p k n", p=128),
                                     nk, 512))

        for _ in range(ntiles):
            plan_tile()
        wst = dict(issued=0, next=0, released=[False] * len(plan))

        def wview(slot, kd, ncol):
            return wr[:, slot, 0:kd * ncol].rearrange("p (k n) -> p k n", k=kd)

        def wpump():
            while wst["issued"] < len(plan):
                n = wst["issued"]
                if n >= NS and not wst["released"][n - NS]:
                    break
                tag, src, kd, ncol = plan[n]
                slot = n % NS
                dst = wview(slot, kd, ncol)
                S.add("pool", lambda e, dst=dst, src=src: e.dma_start(out=dst, in_=src),
                      writes=[("w", slot)], dma_key=("w", slot))
                wst["issued"] += 1

        def wacq(tag):
            n = wst["next"]
            assert plan[n][0] == tag, (plan[n][0], tag)
            wpump()
            assert wst["issued"] > n, "weight ring deadlock"
            wst["next"] += 1
            _, _, kd, ncol = plan[n]
            return n, n % NS, wview(n % NS, kd, ncol)

        def wrel(n):
            wst["released"][n] = True
            wpump()

        tff = tf[:].rearrange("p a b -> p (a b)")
        S.add("sp", lambda e: e.dma_start(out=pp[:], in_=pp_d), writes=["pp"], dma_key="c0")
        S.add("sp", lambda e: e.dma_start(out=tff[:, 0:640], in_=cmf_d), writes=[("tf", 0), ("tf", 1)], dma_key="c1")
        S.add("sp", lambda e: e.dma_start(out=b2[:].rearrange("p a b -> p (a b)"), in_=bs_d), writes=["b2"], dma_key="c2")
        S.add("sp", lambda e: e.dma_start(out=tff[:, 1024:2048], in_=wT_d), writes=[("tf", 2), ("tf", 3)], dma_key="c3")
        S.add("sp", lambda e: e.dma_start(out=tff[:, 2048:2304], in_=lamb_d), writes=[("tf", 4)], dma_key="c4")
        wpump()
        S.add("dve", lambda e: e.tensor_copy(out=cm[:].rearrange("p a b -> p (a b)"), in_=tff[:, 0:640]),
              reads=[("tf", 0), ("tf", 1)], writes=["cm"])
        S.add("dve", lambda e: e.memset(carry[:], 0.0), writes=["carry"])
        S.add("dve", lambda e: e.memset(col(sc, SC_EPS), EPS), writes=["sc_eps"])
        S.add("dve", lambda e: e.memset(sc[:, SC_M0:SC_M1 + 1], 0.0), writes=["sc_m"])
        S.add("dve", lambda e: e.memset(sc[0:64, SC_M0:SC_M0 + 1], 1.0), reads=["sc_m"], writes=["sc_m"])
        S.add("dve", lambda e: e.memset(sc[64:128, SC_M1:SC_M1 + 1], 1.0), reads=["sc_m"], writes=["sc_m"])
        for g in range(8):
            S.add("dve", lambda e, g=g: e.tensor_tensor(out=wTm[:, g, :], in0=tff[:, 1024 + 128 * g:1024 + 128 * g + 128],
                                                        in1=tff[:, TRI * 128:TRI * 128 + 128], op=ALU.mult),
                  reads=[("tf", 0), ("tf", 1), ("tf", 2), ("tf", 3)], writes=["wTm"])
        for g in range(8):
            bk = nb()
            S.add("pe", lambda e, g=g, bk=bk: e.matmul(ps[:, bk, 0:128], lhsT=cm[:, ONES, :], rhs=wTm[:, g, :], start=True, stop=True),
                  reads=["cm", "wTm"], writes=[PS(bk)])
            S.add("dve", lambda e, g=g, bk=bk: e.scalar_tensor_tensor(out=b2[:, g, :], in0=ps[:, bk, 0:128], scalar=col(pp, PP_LNB + g),
                                                                      in1=b2[:, g, :], op0=ALU.mult, op1=ALU.add),
                  reads=[PS(bk), "pp", "b2"], writes=["b2"])
        S.add("dve", lambda e: e.tensor_scalar(out=hb[:], in0=pp[:, PP_BG:PP_BG + 32], scalar1=0.5, scalar2=None, op0=ALU.mult),
              reads=["pp"], writes=["hb"])
        S.add("dve", lambda e: e.tensor_scalar(out=sc[:, SC_GQA:SC_GQB + 1], in0=pp[:, PP_GQA:PP_GQB + 1], scalar1=0.125, scalar2=None, op0=ALU.mult),
              reads=["pp"], writes=["sc_g"])
        S.add("dve", lambda e: e.tensor_copy(out=sc[:, SC_GKA:SC_GKB + 1], in_=pp[:, PP_GKA:PP_GKB + 1]),
              reads=["pp"], writes=["sc_g"])
        S.add("dve", lambda e: e.tensor_scalar(out=col(sc, SC_SUBG), in0=col(pp, PP_SUBG), scalar1=0.8, scalar2=None, op0=ALU.mult),
              reads=["pp"], writes=["sc_g"])
        S.add("dve", lambda e: e.tensor_tensor(out=tff[:, 2048:2112], in0=tff[:, 2048:2112], in1=tff[:, 2112:2176], op=ALU.mult),
              reads=[("tf", 4)], writes=[("tf", 4)])
        S.add("dve", lambda e: e.tensor_tensor(out=tff[:, 2176:2240], in0=tff[:, 2176:2240], in1=tff[:, 2240:2304], op=ALU.mult),
              reads=[("tf", 4)], writes=[("tf", 4)])
        S.add("dve", lambda e: e.reduce_sum(out=col(sc, SC_T0), in_=tff[:, 2048:2112], axis=mybir.AxisListType.X),
              reads=[("tf", 4)], writes=["sc_t"])
        S.add("dve", lambda e: e.reduce_sum(out=col(sc, SC_T1), in_=tff[:, 2176:2240], axis=mybir.AxisListType.X),
              reads=[("tf", 4)], writes=["sc_t"])
        S.add("act", lambda e: e.activation(out=sc[:, SC_T0:SC_T1 + 1], in_=sc[:, SC_T0:SC_T1 + 1], func=AF.Exp),
              reads=["sc_t"], writes=["sc_t"])
        S.add("dve", lambda e: e.tensor_tensor(out=col(sc, SC_T2), in0=col(sc, SC_T1), in1=col(sc, SC_T0), op=ALU.subtract),
              reads=["sc_t"], writes=["sc_t2"])
        S.add("dve", lambda e: e.tensor_scalar(out=col(sc, SC_NLAM), in0=col(sc, SC_T2), scalar1=-0.2, scalar2=None, op0=ALU.add),
              reads=["sc_t2"], writes=["sc_nlam"])
        SCALL = ["sc_g", "sc_nlam", "sc_eps"]

        def norm_to_T(gcol0, tagrd):
            for c in range(4):
                junk = ar[:, 40:44, :].rearrange("p a b -> p (a b)")
                S.add("act", lambda e, c=c, junk=junk: e.activation(out=junk, in_=hh[:, c, :], func=AF.Square, accum_out=col(stt, c)),
                      reads=HK(c), writes=[SL(40), SL(41), SL(42), SL(43), ("stt", c)])
                S.add("act", lambda e, c=c: e.activation(out=col(stt, 4 + c), in_=col(stt, c), func=AF.Ln, scale=1.0 / D, bias=col(sc, SC_EPS)),
                      reads=[("stt", c), "sc_eps"], writes=[("stt", 4 + c)])
                S.add("act", lambda e, c=c: e.activation(out=col(stt, 8 + c), in_=col(stt, 4 + c), func=AF.Exp, scale=-0.5),
                      reads=[("stt", 4 + c)], writes=[("stt", 8 + c)])
                s0 = 32 + 4 * (c % 2)
                stg = ar[:, s0:s0 + 4, :].rearrange("p a b -> p (a b)")
                S.add("dve", lambda e, c=c, stg=stg: e.tensor_scalar(out=stg, in0=hh[:, c, :], scalar1=col(stt, 8 + c), scalar2=None, op0=ALU.mult),
                      reads=HK(c) + [("stt", 8 + c)], writes=[SL(s0 + q) for q in range(4)])
                for G in range(4):
                    bk = nb()
                    pv = ps[:, bk, :].bitcast(BF16)
                    for q in range(4):
                        kc = 4 * G + q
                        S.add("pe", lambda e, pv=pv, q=q, kc=kc, stg=stg: e.transpose(out=pv[:, q * 128:(q + 1) * 128], in_=stg[:, kc * 128:(kc + 1) * 128],
                                                                                     identity=cm[:, IDENT, :]),
                              reads=[SL(s0 + kc // 4), "cm"], writes=[PS(bk)])
                    gb = pp[:, gcol0 + 4 * G:gcol0 + 4 * G + 4].unsqueeze(2).to_broadcast([128, 4, 128])
                    S.add("dve", lambda e, pv=pv, G=G, c=c, gb=gb: e.tensor_tensor(
                        out=ar[:, 4 * G:4 * G + 4, c * 128:(c + 1) * 128],
                        in0=pv[:, 0:512].rearrange("p (a b) -> p a b", a=4), in1=gb, op=ALU.mult),
                          reads=[PS(bk), "pp"], writes=[SL(4 * G + q) for q in range(4)])

        def dump(name, ap, shape, reads):
            d = nc.dram_tensor("dbg_" + name, list(shape), ap.dtype, kind="ExternalOutput").ap()
            dbg_d[name] = d
            S.add("sp", lambda e: e.dma_start(out=d, in_=ap), reads=reads, dma_key="dbg_" + name)

        ALLSL = [SL(i) for i in range(48)]

        for ti in range(ntiles if stop_after != "setup" else 0):
            tok0 = ti * T
            for c in range(4):
                S.add("sp", lambda e, c=c, tok0=tok0: e.dma_start(out=hh[:, c, :], in_=x_d[tok0 + c * 128:tok0 + (c + 1) * 128, :]),
                      writes=HK(c), dma_key=("x", c))
            S.add("sp", lambda e, tok0=tok0: e.dma_start(out=cs[:, 0, :], in_=cos_d[:, tok0:tok0 + T]), writes=["cos"], dma_key="cos")
            S.add("sp", lambda e, tok0=tok0: e.dma_start(out=cs[:, 1, :], in_=sin_d[:, tok0:tok0 + T]), writes=["sin"], dma_key="sin")
            norm_to_T(PP_N1G, "n1")
            if stop_after == "norm1":
                break

            pend = []

            def qk_post2(bz, zsq, zb, is_q, hd):
                bs_ = nb()
                br_ = nb()
                S.add("pe", lambda e: e.matmul(ps[:, bs_, :], lhsT=cm[:, BONES, :], rhs=tb[:, zsq, :], start=True, stop=True),
                      reads=["cm", ("tb", zsq)], writes=[PS(bs_)])
                S.add("pe", lambda e: e.matmul(ps[:, br_, :], lhsT=cm[:, ROT, :], rhs=tb[:, zb, :], start=True, stop=True),
                      reads=["cm", ("tb", zb)], writes=[PS(br_)])
                rs = ntf()
                t1 = ntf()
                t2 = ntf()
                S.add("act", lambda e: e.activation(out=tf[:, rs, :], in_=ps[:, bs_, :], func=AF.Ln, scale=1.0 / 64, bias=col(sc, SC_EPS)),
                      reads=[PS(bs_), "sc_eps"], writes=[("tf", rs)])
                S.add("act", lambda e: e.activation(out=tf[:, rs, :], in_=tf[:, rs, :], func=AF.Exp, scale=-0.5),
                      reads=[("tf", rs)], writes=[("tf", rs)])
                ga = SC_GQA if is_q else SC_GKA
                S.add("dve", lambda e: e.scalar_tensor_tensor(out=tf[:, t1, :], in0=ps[:, bz, :], scalar=col(sc, ga), in1=cs[:, 0, :],
                                                              op0=ALU.mult, op1=ALU.mult),
                      reads=[PS(bz), "cos"] + SCALL, writes=[("tf", t1)])
                S.add("dve", lambda e: e.scalar_tensor_tensor(out=tf[:, t2, :], in0=ps[:, br_, :], scalar=col(sc, ga + 1), in1=cs[:, 1, :],
                                                              op0=ALU.mult, op1=ALU.mult),
                      reads=[PS(br_), "sin"] + SCALL, writes=[("tf", t2)])
                S.add("dve", lambda e: e.tensor_tensor(out=tf[:, t1, :], in0=tf[:, t1, :], in1=tf[:, t2, :], op=ALU.add),
                      reads=[("tf", t1), ("tf", t2)], writes=[("tf", t1)])
                if is_q:
                    dst = ar[:, 16 + hd, :]
                    wk = [SL(16 + hd)]
                else:
                    dst = kT[:, hd, tok0:tok0 + T]
                    wk = [("kT", hd, ti)]
                S.add("dve", lambda e: e.tensor_tensor(out=dst, in0=tf[:, t1, :], in1=tf[:, rs, :], op=ALU.mult),
                      reads=[("tf", t1), ("tf", rs)], writes=wk)

            for is_q, nm in ((True, "q"), (False, "k")):
                for b in range(4):
                    n, slot, wv = wacq(f"{nm}{b}")
                    for half in range(2):
                        hd = 2 * b + half
                        bz = nb()
                        for kc in range(16):
                            S.add("pe", lambda e, bz=bz, kc=kc, wv=wv, half=half: e.matmul(
                                ps[:, bz, :], lhsT=wv[:, kc, half * 128:(half + 1) * 128], rhs=ar[:, kc, :], start=(kc == 0), stop=(kc == 15)),
                                  reads=[("w", slot), SL(kc)], writes=[PS(bz)])
                        zsq = ntb()
                        zb = ntb()
                        S.add("act", lambda e, bz=bz, zsq=zsq: e.activation(out=tb[:, zsq, :], in_=ps[:, bz, :], func=AF.Square),
                              reads=[PS(bz)], writes=[("tb", zsq)])
                        S.add("act", lambda e, bz=bz, zb=zb: e.activation(out=tb[:, zb, :], in_=ps[:, bz, :], func=AF.Copy),
                              reads=[PS(bz)], writes=[("tb", zb)])
                        for f in pend:
                            f()
                        pend = [lambda bz=bz, zsq=zsq, zb=zb, is_q=is_q, hd=hd: qk_post2(bz, zsq, zb, is_q, hd)]
                    wrel(n)
            for b in range(4):
                n, slot, wv = wacq(f"v{b}")
                for c in range(4):
                    bk = nb()
                    for kc in range(16):
                        S.add("pe", lambda e, bk=bk, kc=kc, wv=wv, c=c: e.matmul(
                            ps[:, bk, 0:256], lhsT=ar[:, kc, c * 128:(c + 1) * 128], rhs=wv[:, kc, :], start=(kc == 0), stop=(kc == 15)),
                              reads=[("w", slot), SL(kc)], writes=[PS(bk)])
                    if b == 0 and c == 0:
                        for f in pend:
                            f()
                        pend = []
                    dst = vv[:, 4 * ti + c, b * 256:(b + 1) * 256]
                    if c % 2 == 0:
                        S.add("act", lambda e, bk=bk, dst=dst: e.activation(out=dst, in_=ps[:, bk, 0:256], func=AF.Copy),
                              reads=[PS(bk)], writes=[("v", 4 * ti + c)])
                    else:
                        S.add("dve", lambda e, bk=bk, dst=dst: e.tensor_copy(out=dst, in_=ps[:, bk, 0:256]),
                              reads=[PS(bk)], writes=[("v", 4 * ti + c)])
                wrel(n)
            for b in range(4):
                n, slot, wv = wacq(f"u{b}")
                for half in range(2):
                    g = 2 * b + half
                    bk = nb()
                    for kc in range(16):
                        S.add("pe", lambda e, bk=bk, kc=kc, wv=wv, half=half: e.matmul(
                            ps[:, bk, :], lhsT=wv[:, kc, half * 128:(half + 1) * 128], rhs=ar[:, kc, :], start=(kc == 0), stop=(kc == 15)),
                              reads=[("w", slot), SL(kc)], writes=[PS(bk)])
                    S.add("act", lambda e, bk=bk, g=g: e.activation(out=ar[:, 24 + g, :], in_=ps[:, bk, :], func=AF.Gelu),
                          reads=[PS(bk)], writes=[SL(24 + g)])
                wrel(n)
            S.add("dve", lambda e: e.memset(stt[:, 16:48], 0.0), writes=["stt_sv"])

            def vn_ap(c, c0, c1):
                return ar[:, 32 + 2 * c:34 + 2 * c, :].rearrange("p a b -> p (a b)")[:, c0:c1]

            for b in range(4):
                n, slot, wv = wacq(f"sv{b}")
                for c in range(4):
                    bk = nb()
                    for kc in range(16):
                        S.add("pe", lambda e, bk=bk, kc=kc, wv=wv, c=c: e.matmul(
                            ps[:, bk, 0:256], lhsT=ar[:, kc, c * 128:(c + 1) * 128], rhs=wv[:, kc, :], start=(kc == 0), stop=(kc == 15)),
                              reads=[("w", slot), SL(kc)], writes=[PS(bk)])
                    dst = vn_ap(c, b * 256, (b + 1) * 256)
                    S.add("act", lambda e, bk=bk, dst=dst, c=c, b=b: e.activation(out=dst, in_=ps[:, bk, 0:256], func=AF.Gelu,
                                                                              accum_out=col(stt, 16 + 4 * c + b)),
                          reads=[PS(bk), "stt_sv"], writes=[SL(32 + 2 * c), SL(33 + 2 * c), ("sts", c, b)])
                    jk = ntb()
                    S.add("act", lambda e, dst=dst, jk=jk, c=c, b=b: e.activation(out=tb[:, jk, 0:256], in_=dst, func=AF.Square,
                                                                              accum_out=col(stt, 32 + 4 * c + b)),
                          reads=[SL(32 + 2 * c), SL(33 + 2 * c), "stt_sv"], writes=[("tb", jk), ("stq", c, b)])
                wrel(n)
            def sv_stats_a():
                rd = [("sts", c, b) for c in range(4) for b in range(4)] + [("stq", c, b) for c in range(4) for b in range(4)]
                X = mybir.AxisListType.X
                S.add("dve", lambda e: e.reduce_sum(out=stt[:, 48:52], in_=stt[:, 16:32].rearrange("p (c b) -> p c b", b=4), axis=X),
                      reads=rd, writes=["svA"])
                S.add("dve", lambda e: e.reduce_sum(out=stt[:, 52:56], in_=stt[:, 32:48].rearrange("p (c b) -> p c b", b=4), axis=X),
                      reads=rd + ["svA"], writes=["svA"])
                S.add("dve", lambda e: e.tensor_scalar(out=stt[:, 48:52], in0=stt[:, 48:52], scalar1=1.0 / 1024, scalar2=None, op0=ALU.mult),
                      reads=["svA"], writes=["svA"])
                S.add("dve", lambda e: e.tensor_tensor(out=stt[:, 56:60], in0=stt[:, 48:52], in1=stt[:, 48:52], op=ALU.mult),
                      reads=["svA"], writes=["svA"])
                S.add("dve", lambda e: e.scalar_tensor_tensor(out=stt[:, 56:60], in0=stt[:, 52:56], scalar=1.0 / 1024, in1=stt[:, 56:60],
                                                              op0=ALU.mult, op1=ALU.subtract),
                      reads=["svA"], writes=["svA"])

            def sv_stats_b():
                S.add("act", lambda e: e.activation(out=stt[:, 60:64], in_=stt[:, 56:60], func=AF.Ln, bias=col(sc, SC_EPS)),
                      reads=["svA", "sc_eps"], writes=["svB"])
                S.add("act", lambda e: e.activation(out=stt[:, 60:64], in_=stt[:, 60:64], func=AF.Exp, scale=-0.5),
                      reads=["svB"], writes=["svB"])
                for c in range(4):
                    vfull = vn_ap(c, 0, 1024)
                    S.add("dve", lambda e, c=c, vfull=vfull: e.tensor_scalar(out=vfull, in0=vfull, scalar1=col(stt, 48 + c), scalar2=col(stt, 60 + c),
                                                                             op0=ALU.subtract, op1=ALU.mult),
                          reads=[SL(32 + 2 * c), SL(33 + 2 * c), "svA", "svB"], writes=[SL(32 + 2 * c), SL(33 + 2 * c)])

            sv_stats_a()
            if stop_after == "proj":
                sv_stats_b()
                break

            nj = 4 * (ti + 1)
            late = []
            def qz_prep(hq):
                z0 = 40 + 2 * (hq % 4)
                S.add("dve", lambda e: e.tensor_scalar(out=ar[:, z0, :], in0=ar[:, 16 + hq, :], scalar1=col(sc, SC_M0), scalar2=None, op0=ALU.mult),
                      reads=[SL(16 + hq), "sc_m"], writes=[SL(z0)])
                S.add("dve", lambda e: e.tensor_scalar(out=ar[:, z0 + 1, :], in0=ar[:, 16 + hq, :], scalar1=col(sc, SC_M1), scalar2=None, op0=ALU.mult),
                      reads=[SL(16 + hq), "sc_m"], writes=[SL(z0 + 1)])

            qz_prep(0)
            st_ = dict(carry=None, part2=None, part3=None)

            def make_tail(hd, last_av):
                slots = {}

                def part1():
                    for f in last_av:
                        f()
                    r0, rb, rc, r1 = ntf(), ntf(), ntf(), ntf()
                    slots.update(r0=r0, rb=rb, rc=rc, r1=r1)
                    S.add("act", lambda e: e.activation(out=tf[:, r0, :], in_=ps[:, 4, :], func=AF.Copy), reads=[PS(4)], writes=[("tf", r0)])
                    S.add("dve", lambda e: e.tensor_copy(out=tf[:, rb, :], in_=ps[:, 5, :]), reads=[PS(5)], writes=[("tf", rb)])
                    S.add("act", lambda e: e.activation(out=tf[:, rc, :], in_=ps[:, 6, :], func=AF.Ln), reads=[PS(6)], writes=[("tf", rc)])
                    S.add("dve", lambda e: e.tensor_copy(out=tf[:, r1, :], in_=ps[:, 7, :]), reads=[PS(7)], writes=[("tf", r1)])

                def part2():
                    r0, rb, rc, r1 = slots["r0"], slots["rb"], slots["rc"], slots["r1"]
                    S.add("act", lambda e: e.activation(out=tf[:, rc, :], in_=tf[:, rc, :], func=AF.Exp, scale=-1.0),
                          reads=[("tf", rc)], writes=[("tf", rc)])
                    S.add("act", lambda e: e.activation(out=tf[:, r1, :], in_=tf[:, r1, :], func=AF.Ln), reads=[("tf", r1)], writes=[("tf", r1)])
                    S.add("act", lambda e: e.activation(out=tf[:, r1, :], in_=tf[:, r1, :], func=AF.Exp, scale=-1.0),
                          reads=[("tf", r1)], writes=[("tf", r1)])
                    S.add("dve", lambda e: e.tensor_tensor(out=tf[:, r0, :], in0=tf[:, r0, :], in1=tf[:, rc, :], op=ALU.mult),
                          reads=[("tf", r0), ("tf", rc)], writes=[("tf", r0)])
                    S.add("dve", lambda e: e.tensor_tensor(out=tf[:, rb, :], in0=tf[:, rb, :], in1=tf[:, r1, :], op=ALU.mult),
                          reads=[("tf", rb), ("tf", r1)], writes=[("tf", rb)])
                    S.add("dve", lambda e: e.scalar_tensor_tensor(out=tf[:, r0, :], in0=tf[:, rb, :], scalar=col(sc, SC_NLAM), in1=tf[:, r0, :],
                                                                  op0=ALU.mult, op1=ALU.add),
                          reads=[("tf", r0), ("tf", rb)] + SCALL, writes=[("tf", r0)])
                    if hd == 0:
                        sv_stats_b()

                def part3():
                    r0 = slots["r0"]
                    S.add("act", lambda e: e.activation(out=osqb[:], in_=tf[:, r0, :], func=AF.Square),
                          reads=[("tf", r0)], writes=["osq"])

                def subln():
                    r0, r1 = slots["r0"], slots["r1"]
                    pr = cnt["pair"] % 2
                    cnt["pair"] += 1
                    bk = 2 * pr
                    S.add("pe", lambda e: e.matmul(ps[:, bk, :], lhsT=cm[:, ONES, :], rhs=osqb[:], start=True, stop=True),
                          reads=["cm", "osq"], writes=[PS(bk)])
                    S.add("act", lambda e: e.activation(out=tf[:, r1, :], in_=ps[:, bk, :], func=AF.Ln, scale=1.0 / 128, bias=col(sc, SC_EPS)),
                          reads=[PS(bk), "sc_eps"], writes=[("tf", r1)])
                    S.add("act", lambda e: e.activation(out=tf[:, r1, :], in_=tf[:, r1, :], func=AF.Exp, scale=-0.5),
                          reads=[("tf", r1)], writes=[("tf", r1)])
                    S.add("dve", lambda e: e.scalar_tensor_tensor(out=ar[:, 16 + hd, :], in0=tf[:, r0, :], scalar=col(sc, SC_SUBG), in1=tf[:, r1, :],
                                                                  op0=ALU.mult, op1=ALU.mult),
                          reads=[("tf", r0), ("tf", r1)] + SCALL, writes=[SL(16 + hd)])
                return part1, part2, part3, subln

            late = []
            for hd in range(8):
                pav = []
                qz0 = 40 + 2 * (hd % 4)
                qz1 = qz0 + 1
                for j in range(nj):
                    r = j - 4 * ti
                    c0 = 128 * r if r > 0 else 0
                    pr = cnt["pair"] % 2
                    cnt["pair"] += 1
                    sb0, sb1 = 2 * pr, 2 * pr + 1
                    ksl = slice(j * 128, (j + 1) * 128)
                    kkey = ("kT", hd, j // 4)
                    S.add("pe", lambda e, sb0=sb0, c0=c0, ksl=ksl, hd=hd, qz0=qz0: e.matmul(
                        ps[:, sb0, c0:512], lhsT=kT[:, hd, ksl], rhs=ar[:, qz0, c0:512], start=True, stop=True),
                          reads=[kkey, SL(qz0)], writes=[PS(sb0)])
                    S.add("pe", lambda e, sb1=sb1, c0=c0, ksl=ksl, hd=hd, qz1=qz1: e.matmul(
                        ps[:, sb1, c0:512], lhsT=kT[:, hd, ksl], rhs=ar[:, qz1, c0:512], start=True, stop=True),
                          reads=[kkey, SL(qz1)], writes=[PS(sb1)])
                    if j == 0 and hd < 7:
                        qz_prep(hd + 1)
                    p0 = ntb()
                    p1 = ntb()
                    S.add("act", lambda e, sb0=sb0, p0=p0, c0=c0: e.activation(out=tb[:, p0, c0:512], in_=ps[:, sb0, c0:512], func=AF.Exp),
                          reads=[PS(sb0)], writes=[("tb", p0)])
                    S.add("act", lambda e, sb1=sb1, p1=p1, c0=c0: e.activation(out=tb[:, p1, c0:512], in_=ps[:, sb1, c0:512], func=AF.Exp),
                          reads=[PS(sb1)], writes=[("tb", p1)])
                    if r >= 0:
                        for pq in (p0, p1):
                            S.add("dve", lambda e, pq=pq, c0=c0: e.tensor_tensor(out=tb[:, pq, c0:c0 + 128], in0=tb[:, pq, c0:c0 + 128],
                                                                                 in1=cm[:, TRI, :], op=ALU.mult),
                                  reads=[("tb", pq), "cm"], writes=[("tb", pq)])
                    if j == 0 and st_["carry"] is not None:
                        st_["carry"]()
                        st_["carry"] = None
                    for f in pav:
                        f()

                    def av(j=j, c0=c0, p0=p0, p1=p1, hd=hd, nj=nj):
                        vsl = vv[:, j, hd * 128:(hd + 1) * 128]
                        for (bo, bsum, pq) in ((4, 6, p0), (5, 7, p1)):
                            S.add("pe", lambda e, bo=bo, pq=pq, vsl=vsl: e.matmul(
                                ps[:, bo, c0:512], lhsT=vsl, rhs=tb[:, pq, c0:512], start=(j == 0), stop=(j == nj - 1)),
                                  reads=[("v", j), ("tb", pq)], writes=[PS(bo)])
                            S.add("pe", lambda e, bsum=bsum, pq=pq: e.matmul(
                                ps[:, bsum, c0:512], lhsT=cm[:, ONES, :], rhs=tb[:, pq, c0:512], start=(j == 0), stop=(j == nj - 1)),
                                  reads=["cm", ("tb", pq)], writes=[PS(bsum)])
                    pav = [av]
                    if j == 1 and st_["part2"] is not None:
                        st_["part2"]()
                        st_["part2"] = None
                    if j == min(3, nj - 1) and st_["part3"] is not None:
                        st_["part3"]()
                        st_["part3"] = None
                    if j == min(6, nj - 1):
                        for f in late:
                            f()
                        late = []
                assert not late and st_["part2"] is None and st_["part3"] is None
                p1_, p2_, p3_, sub_ = make_tail(hd, pav)
                st_["carry"], st_["part2"], st_["part3"] = p1_, p2_, p3_
                late = [sub_]
            st_["carry"]()
            st_["part2"]()
            st_["part3"]()
            st_["carry"] = st_["part2"] = st_["part3"] = None

            for g in range(8):
                bk = nb()
                for c in range(4):
                    S.add("pe", lambda e, bk=bk, c=c, g=g: e.matmul(
                        ps[:, bk, c * 128:(c + 1) * 128], lhsT=vn_ap(c, g * 128, (g + 1) * 128), rhs=wTm[:, g, :], start=True, stop=True),
                          reads=[SL(32 + 2 * c), SL(33 + 2 * c), "wTm"], writes=[PS(bk)])
                b2b = b2[:, g, :].unsqueeze(1).broadcast_to([128, 4, 128])
                psv = ps[:, bk, :].rearrange("p (a b) -> p a b", a=4)
                S.add("dve", lambda e, psv=psv, g=g, b2b=b2b: e.scalar_tensor_tensor(
                    out=psv, in0=psv, scalar=col(pp, PP_LNG + g), in1=b2b, op0=ALU.mult, op1=ALU.add),
                      reads=[PS(bk), "pp", "b2"], writes=[PS(bk)])
                S.add("dve", lambda e, g=g, bk=bk: e.tensor_tensor(out=ar[:, 24 + g, :], in0=ps[:, bk, :], in1=ar[:, 24 + g, :], op=ALU.mult),
                      reads=[PS(bk), SL(24 + g)], writes=[SL(24 + g)])
            if stop_after == "mix":
                for f in late:
                    f()
                late = []
                break

            for jp in range(8):
                res = {}
                for nm, kd, src0 in (("ga", 16, 0), ("gs", 16, 0), ("att", 8, 16), ("sgu", 8, 24)):
                    n, slot, wv = wacq(f"{nm}{jp}")
                    for half in range(2):
                        bk = nb()
                        for kc in range(kd):
                            S.add("pe", lambda e, bk=bk, kc=kc, wv=wv, half=half, kd=kd, src0=src0: e.matmul(
                                ps[:, bk, :], lhsT=wv[:, kc, half * 128:(half + 1) * 128], rhs=ar[:, src0 + kc, :], start=(kc == 0), stop=(kc == kd - 1)),
                                  reads=[("w", slot), SL(src0 + kc)], writes=[PS(bk)])
                        res[(nm, half)] = bk
                        if late and half == 1:
                            for f in late:
                                f()
                            late = []
                        if nm in ("ga", "gs"):
                            j = 2 * jp + half
                            bcol = j if nm == "ga" else 16 + j
                            t = ntf()
                            S.add("act", lambda e, bk=bk, t=t, bcol=bcol: e.activation(out=tf[:, t, :], in_=ps[:, bk, :], func=AF.Tanh, scale=0.5,
                                                                                   bias=col(hb, bcol)),
                                  reads=[PS(bk), "hb"], writes=[("tf", t)])
                            res[(nm + "t", half)] = t
                    wrel(n)
                for half in range(2):
                    j = 2 * jp + half
                    ta, ts_ = res[("gat", half)], res[("gst", half)]
                    ba, bs_ = res[("att", half)], res[("sgu", half)]
                    S.add("dve", lambda e, ta=ta, ba=ba: e.scalar_tensor_tensor(out=tf[:, ta, :], in0=tf[:, ta, :], scalar=1.0, in1=ps[:, ba, :],
                                                                                op0=ALU.add, op1=ALU.mult),
                          reads=[("tf", ta), PS(ba)], writes=[("tf", ta)])
                    S.add("dve", lambda e, ts_=ts_, bs_=bs_: e.scalar_tensor_tensor(out=tf[:, ts_, :], in0=tf[:, ts_, :], scalar=1.0, in1=ps[:, bs_, :],
                                                                                    op0=ALU.add, op1=ALU.mult),
                          reads=[("tf", ts_), PS(bs_)], writes=[("tf", ts_)])
                    S.add("dve", lambda e, ta=ta, ts_=ts_, j=j: e.tensor_tensor(out=ar[:, 32 + j, :], in0=tf[:, ta, :], in1=tf[:, ts_, :], op=ALU.add),
                          reads=[("tf", ta), ("tf", ts_)], writes=[SL(32 + j)])
            for b in range(8):
                n, slot, wv = wacq(f"wo{b}")
                for c in range(4):
                    bk = nb()
                    for kc in range(16):
                        S.add("pe", lambda e, bk=bk, kc=kc, wv=wv, c=c: e.matmul(
                            ps[:, bk, 0:256], lhsT=ar[:, 32 + kc, c * 128:(c + 1) * 128], rhs=wv[:, kc, :], start=(kc == 0), stop=(kc == 15)),
                              reads=[("w", slot), SL(32 + kc)], writes=[PS(bk)])
                    hs = hh[:, c, b * 256:(b + 1) * 256]
                    S.add("dve", lambda e, bk=bk, hs=hs: e.scalar_tensor_tensor(out=hs, in0=ps[:, bk, 0:256], scalar=0.5, in1=hs, op0=ALU.mult, op1=ALU.add),
                          reads=[PS(bk), ("h", c, b // 2)], writes=[("h", c, b // 2)])
                wrel(n)
            if stop_after == "mixer":
                break
            norm_to_T(PP_N2G, "n2")
            for g in range(3):
                nch = 16 if g < 2 else 12
                for pr in range(nch // 2):
                    j0 = 16 * g + 2 * pr
                    ng, sg, wg = wacq(f"upg{j0}")
                    nv, sv_, wvv = wacq(f"upv{j0}")
                    for half in range(2):
                        jj = j0 + half
                        tt = {}
                        for kind, slot, wv, chn in (("g", sg, wg, jj), ("v", sv_, wvv, NHC + jj)):
                            bk = nb()
                            for kc in range(16):
                                S.add("pe", lambda e, bk=bk, kc=kc, wv=wv, half=half: e.matmul(
                                    ps[:, bk, :], lhsT=wv[:, kc, half * 128:(half + 1) * 128], rhs=ar[:, kc, :], start=(kc == 0), stop=(kc == 15)),
                                      reads=[("w", slot), SL(kc)], writes=[PS(bk)])
                            t0 = ntf()
                            w0 = col(pp, PP_CW + 3 * chn + 0)
                            w1 = col(pp, PP_CW + 3 * chn + 1)
                            w2 = col(pp, PP_CW + 3 * chn + 2)
                            cb_ = col(pp, PP_CB + chn)
                            ck = ("carry", chn)
                            S.add("act", lambda e, bk=bk, t0=t0, w2=w2, cb_=cb_: e.activation(out=tf[:, t0, :], in_=ps[:, bk, :], func=AF.Identity,
                                                                                          scale=w2, bias=cb_),
                                  reads=[PS(bk), "pp"], writes=[("tf", t0)])
                            S.add("dve", lambda e, bk=bk, t0=t0, w1=w1: e.scalar_tensor_tensor(
                                out=tf[:, t0, 1:512], in0=ps[:, bk, 0:511], scalar=w1, in1=tf[:, t0, 1:512], op0=ALU.mult, op1=ALU.add),
                                  reads=[PS(bk), ("tf", t0), "pp"], writes=[("tf", t0)])
                            S.add("dve", lambda e, bk=bk, t0=t0, w0=w0: e.scalar_tensor_tensor(
                                out=tf[:, t0, 2:512], in0=ps[:, bk, 0:510], scalar=w0, in1=tf[:, t0, 2:512], op0=ALU.mult, op1=ALU.add),
                                  reads=[PS(bk), ("tf", t0), "pp"], writes=[("tf", t0)])
                            S.add("dve", lambda e, t0=t0, w0=w0, chn=chn: e.scalar_tensor_tensor(
                                out=tf[:, t0, 0:2], in0=carry[:, chn, :], scalar=w0, in1=tf[:, t0, 0:2], op0=ALU.mult, op1=ALU.add),
                                  reads=[ck, ("tf", t0), "pp", "carry"], writes=[("tf", t0)])
                            S.add("dve", lambda e, t0=t0, w1=w1, chn=chn: e.scalar_tensor_tensor(
                                out=tf[:, t0, 0:1], in0=carry[:, chn, 1:2], scalar=w1, in1=tf[:, t0, 0:1], op0=ALU.mult, op1=ALU.add),
                                  reads=[ck, ("tf", t0), "pp", "carry"], writes=[("tf", t0)])
                            S.add("dve", lambda e, bk=bk, chn=chn: e.tensor_copy(out=carry[:, chn, :], in_=ps[:, bk, 510:512]),
                                  reads=[PS(bk), "carry"], writes=[ck])
                            tt[kind] = t0
                        gl = ntb()
                        S.add("act", lambda e, gl=gl, tg=tt["g"]: e.activation(out=tb[:, gl, :], in_=tf[:, tg, :], func=AF.Gelu),
                              reads=[("tf", tt["g"])], writes=[("tb", gl)])
                        hsl = 16 + (jj - 16 * g)
                        S.add("dve", lambda e, gl=gl, tv=tt["v"], hsl=hsl: e.tensor_tensor(out=ar[:, hsl, :], in0=tb[:, gl, :], in1=tf[:, tv, :], op=ALU.mult),
                              reads=[("tb", gl), ("tf", tt["v"])], writes=[SL(hsl)])
                    wrel(ng)
                    wrel(nv)
                for cb in range(4):
                    nhf = nch // 8 + (1 if nch % 8 else 0)
                    banks = [nb() for _ in range(4)]
                    for hf in range(nhf):
                        n, slot, wv = wacq(f"dn{g}_{cb}_{hf}")
                        nk = min(8, nch - 8 * hf)
                        for c in range(4):
                            for k in range(nk):
                                kl = 8 * hf + k
                                S.add("pe", lambda e, bk=banks[c], k=k, kl=kl, wv=wv, c=c, nch=nch: e.matmul(
                                    ps[:, bk, :], lhsT=ar[:, 16 + kl, c * 128:(c + 1) * 128], rhs=wv[:, k, :], start=(kl == 0), stop=(kl == nch - 1)),
                                      reads=[("w", slot), SL(16 + kl)], writes=[PS(banks[c])])
                        wrel(n)
                    for c in range(4):
                        hs = hh[:, c, cb * 512:(cb + 1) * 512]
                        if g == 2 and cb == 3:
                            stg_ = ntf()
                            S.add("dve", lambda e, bk=banks[c], hs=hs, stg_=stg_: e.tensor_tensor(out=tf[:, stg_, :], in0=ps[:, bk, :], in1=hs, op=ALU.add),
                                  reads=[PS(banks[c]), ("h", c, cb)], writes=[("tf", stg_)])
                            od = out_d[tok0 + c * 128:tok0 + (c + 1) * 128, cb * 512:(cb + 1) * 512]
                            S.add("sp", lambda e, od=od, stg_=stg_: e.dma_start(out=od, in_=tf[:, stg_, :]),
                                  reads=[("tf", stg_)], dma_key=("o", c, cb))
                            continue
                        S.add("dve", lambda e, bk=banks[c], hs=hs: e.tensor_tensor(out=hs, in0=ps[:, bk, :], in1=hs, op=ALU.add),
                              reads=[PS(banks[c]), ("h", c, cb)], writes=[("h", c, cb)])
                        if g == 2:
                            od = out_d[tok0 + c * 128:tok0 + (c + 1) * 128, cb * 512:(cb + 1) * 512]
                            S.add("sp", lambda e, od=od, hs=hs: e.dma_start(out=od, in_=hs),
                                  reads=[("h", c, cb)], dma_key=("o", c, cb))

        if debug:
            dump("kT", kT[:], [128, 8, SEQ], [("kT", h_, t_) for h_ in range(8) for t_ in range(ntiles)])
            dump("vv", vv[:], [128, 16, 1024], [("v", j) for j in range(16)])
            dump("ar", ar[:], [128, 48, 512], ALLSL)
            dump("hh", hh[:], [128, 4, D], [k for c in range(4) for k in HK(c)])
        S.emit(nc)
    return nc, S


def _consts():
    p = np.arange(128)
    ident = np.eye(128, dtype=np.float32)
    rot = np.zeros((128, 128), np.float32)
    for m in range(128):
        d = m % 64
        if d < 32:
            rot[m + 32, m] = -1.0
        else:
            rot[m - 32, m] = 1.0
    bones = (p[:, None] // 64 == p[None, :] // 64).astype(np.float32)
    tri = (p[None, :] >= p[:, None]).astype(np.float32)
    ones = np.ones((128, 128), np.float32)
    cmf = np.stack([ident, rot, bones, tri, ones], axis=1).reshape(128, 5 * 128)
    inv = np.exp(-math.log(10000.0) * np.arange(0, 64, 2, dtype=np.float32) / 64).astype(np.float32)
    ang = np.arange(SEQ, dtype=np.float32)[:, None] * inv[None, :]
    idx = (p % 64) % 32
    cosT = np.cos(ang).astype(np.float32).T[idx]
    sinT = np.sin(ang).astype(np.float32).T[idx]
    return np.ascontiguousarray(cmf), np.ascontiguousarray(cosT), np.ascontiguousarray(sinT)


def _layout_params(inp):
    f = lambda a: np.asarray(a, dtype=np.float32)
    pp = np.zeros((128, PP_END), np.float32)
    pp[:, PP_N1G:PP_N1G + 16] = f(inp["norm1_g"])[0].reshape(16, 128).T
    pp[:, PP_N2G:PP_N2G + 16] = f(inp["norm2_g"])[0].reshape(16, 128).T
    pp[:, PP_BG:PP_BG + 32] = f(inp["b_gate"])[0].reshape(32, 128).T
    d = np.arange(128) % 64
    qg = f(inp["q_norm_g"])[0]
    kg = f(inp["k_norm_g"])[0]
    pp[:, PP_GQA] = qg[d]
    pp[:, PP_GQB] = qg[(d + 32) % 64]
    pp[:, PP_GKA] = kg[d]
    pp[:, PP_GKB] = kg[(d + 32) % 64]
    pp[:, PP_SUBG] = f(inp["subln_g"])[0]
    pp[:, PP_LNG:PP_LNG + 8] = f(inp["sgu_norm_g"])[0].reshape(8, 128).T
    pp[:, PP_LNB:PP_LNB + 8] = f(inp["sgu_norm_b"])[0].reshape(8, 128).T
    cw = f(inp["conv_w"])[0]
    pp[:, PP_CW:PP_CW + 264] = cw.reshape(3, 88, 128).transpose(2, 1, 0).reshape(128, 264)
    pp[:, PP_CB:PP_CB + 88] = f(inp["conv_b"])[0].reshape(88, 128).T
    lamb = np.concatenate([f(inp["lambda_q1"])[0], f(inp["lambda_k1"])[0], f(inp["lambda_q2"])[0], f(inp["lambda_k2"])[0]])
    lamb = np.ascontiguousarray(np.broadcast_to(lamb[None, :], (128, 256)))
    wT = np.ascontiguousarray(f(inp["sgu_w"])[0].transpose(2, 0, 1).reshape(128, 1024))
    bs = np.ascontiguousarray(np.broadcast_to(f(inp["sgu_b"])[0].reshape(1, 1024), (128, 1024)))
    return pp, lamb, wT, bs


_CACHE = {}


def kernel(**inputs):
    f = lambda a: np.ascontiguousarray(np.asarray(a, dtype=np.float32))
    if "nc" not in _CACHE:
        _CACHE["nc"] = build()[0]
    nc = _CACHE["nc"]
    cmf, cosT, sinT = _consts()
    pp, lamb, wT, bs = _layout_params(inputs)
    x = f(inputs["x"])
    shared = {
        "w_in": f(inputs["w_in"])[0], "w_att_out": f(inputs["w_att_out"])[0], "w_sgu_out": f(inputs["w_sgu_out"])[0],
        "w_out": f(inputs["w_out"])[0], "w_up": f(inputs["w_up"])[0], "w_down": f(inputs["w_down"])[0],
        "pp": pp, "lamb": lamb, "cmf": cmf, "sgu_wT": wT, "sgu_bs": bs, "cosT": cosT, "sinT": sinT,
    }
    in_maps = [dict(shared, x=x[b]) for b in range(8)]
    res = run_bass_kernel_spmd(nc, in_maps, core_ids=list(range(8)))
    return np.stack([np.asarray(r["out"], dtype=np.float32) for r in res.results], axis=0)
```
